# Optimizing a Trainium2 kernel written in Bass

```python
import math
import jax, jax.numpy as jnp
from jax import lax
import numpy as np

D_MODEL = 2048
BATCH = 4
SEQ = 8192
DEPTH = 1

MIX_WIDTH = D_MODEL
GMLP_WIDTH = MIX_WIDTH // 2
GMLP_GROUPS = 8
GMLP_DG = GMLP_WIDTH // GMLP_GROUPS
CHUNK = 128
DIFF_WIDTH = MIX_WIDTH - GMLP_WIDTH
DIFF_HEADS = 8
DIFF_DV = DIFF_WIDTH // DIFF_HEADS
DIFF_DK = DIFF_DV // 2
QK_WIDTH = DIFF_HEADS * 2 * DIFF_DK
IN_WIDTH = 2 * GMLP_WIDTH + 2 * QK_WIDTH + DIFF_WIDTH
Q_BLOCK = 128
D_FF = 4 * D_MODEL
EPS = 1e-6

kernel_name = "hybrid_gmlp_diffattn_alibi_block"


def rms_norm(x, g):
    xf = x.astype(jnp.float32)
    y = xf * lax.rsqrt(jnp.mean(xf * xf, axis=-1, keepdims=True) + EPS)
    return (y * g.astype(jnp.float32)).astype(x.dtype)


def layer_norm(x, g, b):
    xf = x.astype(jnp.float32)
    mu = jnp.mean(xf, axis=-1, keepdims=True)
    xc = xf - mu
    var = jnp.mean(xc * xc, axis=-1, keepdims=True)
    y = xc * lax.rsqrt(var + EPS) * g.astype(jnp.float32) + b.astype(jnp.float32)
    return y.astype(x.dtype)


def alibi_slopes(n):
    return jnp.asarray(2.0 ** (-8.0 * np.arange(1, n + 1) / n), dtype=jnp.float32)


def chunked_spatial_gating(z, ln_g, ln_b, w_s, b_s):
    B, S, _ = z.shape
    z = z.reshape(B, S, GMLP_GROUPS, 2, GMLP_DG)
    u, v = z[..., 0, :], z[..., 1, :]
    v = layer_norm(v, ln_g, ln_b)
    causal = jnp.tril(jnp.ones((CHUNK, CHUNK), dtype=bool))
    w = jnp.where(causal, w_s, 0).astype(v.dtype)
    vc = v.reshape(B, S // CHUNK, CHUNK, GMLP_GROUPS, GMLP_DG)
    mixed = jnp.einsum('gts,bcsgd->bctgd', w, vc) + b_s.T[:, :, None].astype(v.dtype)
    return u * mixed.reshape(B, S, GMLP_GROUPS, GMLP_DG)


def diff_attention(q, k, v, lam, slopes):
    B, S = q.shape[0], q.shape[1]
    nb = S // Q_BLOCK
    scale = DIFF_DK ** -0.5
    qb = jnp.moveaxis(q.reshape(B, nb, Q_BLOCK, DIFF_HEADS, 2, DIFF_DK), 1, 0)
    kpos = jnp.arange(S, dtype=jnp.int32)

    def block(args):
        q_blk, start = args
        qpos = start + jnp.arange(Q_BLOCK, dtype=jnp.int32)
        dist = qpos[:, None] - kpos[None, :]
        s = jnp.einsum('bqhmd,bkhmd->bhmqk', q_blk, k).astype(jnp.float32) * scale
        s = s - slopes[None, :, None, None, None] * dist.astype(jnp.float32)
        s = jnp.where(dist >= 0, s, -jnp.inf)
        p = jax.nn.softmax(s, axis=-1)
        a = p[:, :, 0] - lam * p[:, :, 1]
        return jnp.einsum('bhqk,bkhd->bqhd', a.astype(v.dtype), v)

    o = lax.map(block, (qb, jnp.arange(nb, dtype=jnp.int32) * Q_BLOCK))
    return jnp.moveaxis(o, 0, 1).reshape(B, S, DIFF_HEADS, DIFF_DV)


def setup_inputs(seed: int = 0) -> dict:
    key = jax.random.key(seed)
    ks = jax.random.split(key, 20)
    f32 = jnp.float32
    n = lambda k, shape: jax.random.normal(k, shape, dtype=f32)
    L = DEPTH
    return {
        "x": n(ks[0], (BATCH, SEQ, D_MODEL)),
        "pre_mix_g": 1.0 + 0.05 * n(ks[1], (L, D_MODEL)),
        "w_in": n(ks[2], (L, D_MODEL, IN_WIDTH)) * D_MODEL ** -0.5,
        "gmlp_ln_g": 1.0 + 0.05 * n(ks[3], (L, GMLP_GROUPS, GMLP_DG)),
        "gmlp_ln_b": 0.02 * n(ks[4], (L, GMLP_GROUPS, GMLP_DG)),
        "gmlp_w_s": n(ks[5], (L, GMLP_GROUPS, CHUNK, CHUNK)) * CHUNK ** -0.5,
        "gmlp_b_s": 1.0 + 0.1 * n(ks[6], (L, GMLP_GROUPS, CHUNK)),
        "lambda_q1": 0.1 * n(ks[7], (L, DIFF_DK)),
        "lambda_k1": 0.1 * n(ks[8], (L, DIFF_DK)),
        "lambda_q2": 0.1 * n(ks[9], (L, DIFF_DK)),
        "lambda_k2": 0.1 * n(ks[10], (L, DIFF_DK)),
        "diff_subln_g": 1.0 + 0.05 * n(ks[11], (L, DIFF_DV)),
        "w_out": n(ks[12], (L, MIX_WIDTH, D_MODEL)) * MIX_WIDTH ** -0.5,
        "post_mix_g": 1.0 + 0.05 * n(ks[13], (L, D_MODEL)),
        "pre_mlp_g": 1.0 + 0.05 * n(ks[14], (L, D_MODEL)),
        "w_up": n(ks[15], (L, D_MODEL, D_FF)) * D_MODEL ** -0.5,
        "w_down": n(ks[16], (L, D_FF, D_MODEL)) * D_FF ** -0.5,
        "post_mlp_g": 1.0 + 0.05 * n(ks[17], (L, D_MODEL)),
    }


def reference(x, pre_mix_g, w_in, gmlp_ln_g, gmlp_ln_b, gmlp_w_s, gmlp_b_s,
              lambda_q1, lambda_k1, lambda_q2, lambda_k2, diff_subln_g, w_out,
              post_mix_g, pre_mlp_g, w_up, w_down, post_mlp_g):
    B, S, _ = x.shape
    slopes = alibi_slopes(DIFF_HEADS)
    h = x
    for l in range(DEPTH):
        lambda_init = 0.8 - 0.6 * math.exp(-0.3 * l)
        xn = rms_norm(h, pre_mix_g[l])
        proj = xn @ w_in[l]
        z_g, q, k, v = jnp.split(
            proj, [2 * GMLP_WIDTH, 2 * GMLP_WIDTH + QK_WIDTH, 2 * GMLP_WIDTH + 2 * QK_WIDTH], axis=-1)
        y_a = chunked_spatial_gating(jax.nn.gelu(z_g), gmlp_ln_g[l], gmlp_ln_b[l],
                                     gmlp_w_s[l], gmlp_b_s[l]).reshape(B, S, GMLP_WIDTH)
        lam = (jnp.exp(jnp.sum(lambda_q1[l].astype(jnp.float32) * lambda_k1[l].astype(jnp.float32)))
               - jnp.exp(jnp.sum(lambda_q2[l].astype(jnp.float32) * lambda_k2[l].astype(jnp.float32)))
               + lambda_init)
        o = diff_attention(q.reshape(B, S, DIFF_HEADS, 2, DIFF_DK),
                           k.reshape(B, S, DIFF_HEADS, 2, DIFF_DK),
                           v.reshape(B, S, DIFF_HEADS, DIFF_DV), lam, slopes)
        y_b = (rms_norm(o, diff_subln_g[l]) * (1.0 - lambda_init)).reshape(B, S, DIFF_WIDTH)
        y = jnp.concatenate([y_a, y_b], axis=-1) @ w_out[l]
        h = h + rms_norm(y, post_mix_g[l])
        hn = rms_norm(h, pre_mlp_g[l])
        f = jnp.square(jax.nn.relu(hn @ w_up[l])) @ w_down[l]
        h = h + rms_norm(f, post_mlp_g[l])
    return h
```

```python
import numpy as np
import ml_dtypes
import concourse.bass as bass
import concourse.mybir as mybir
from concourse.bass_utils import run_bass_kernel_spmd

F32 = mybir.dt.float32
BF = mybir.dt.bfloat16
AF = mybir.ActivationFunctionType
ALU = mybir.AluOpType
AX = mybir.AxisListType

D = 2048
SEG = 512
NSEG = 8
TOK = 4096
S_FULL = 8192
DFF = 8192
INW = 5120
EPS = 1e-6
OWN = {0: [0, 3, 4, 7, 8, 11, 12, 15], 1: [1, 2, 5, 6, 9, 10, 13, 14]}
SLOPES = [2.0 ** (-(i + 1)) for i in range(8)]
QK_SCALE = 0.125
LAMBDA_INIT = 0.2
NEG_BIG = -1.0e9
DEBUG = False
PHASE_LIMIT = "ABC"
N_SEG_A = 16

ENGS = ("pe", "act", "dve", "pool", "sp")


class Res:
    __slots__ = ("name", "last_w", "readers", "pw")

    def __init__(self, name):
        self.name = name
        self.last_w = None
        self.readers = {}
        self.pw = []


class Op:
    __slots__ = ("eng", "fn", "deps", "signal", "is_dma", "token", "idx", "slotprev")

    def __init__(self, eng, fn, is_dma):
        self.eng = eng
        self.fn = fn
        self.deps = []
        self.signal = False
        self.is_dma = is_dma
        self.token = None
        self.slotprev = None


class Sched:
    def __init__(self, nc, dma_slots=8, sem_limit=30000):
        self.nc = nc
        self.ops = {e: [] for e in ENGS}
        self.dma_slots = dma_slots
        self.sem_limit = sem_limit
        self.n_ops = 0

    def op(self, eng, fn, reads=(), writes=(), dma=False, pwrites=()):
        o = Op(eng, fn, dma)
        o.idx = self.n_ops
        self.n_ops += 1
        deps = []
        for r in reads:
            if r.last_w is not None:
                deps.append((r.last_w, 0))
            for pw in r.pw:
                deps.append((pw, 0))
        for w in writes:
            if w.last_w is not None:
                deps.append((w.last_w, 1))
            for pw in w.pw:
                deps.append((pw, 1))
            for rd in w.readers.values():
                deps.append((rd, 1))
        for w in pwrites:
            for rd in w.readers.values():
                deps.append((rd, 1))
        seen = set()
        for d, kind in deps:
            if d is o or id(d) in seen:
                continue
            if (not d.is_dma) and (not o.is_dma) and d.eng == o.eng:
                if o.eng == "pe" or kind != 0:
                    continue
            seen.add(id(d))
            o.deps.append(d)
            d.signal = True
        key = ("dma", o.idx) if dma else eng
        for r in reads:
            r.readers[key] = o
        for w in writes:
            w.last_w = o
            w.readers = {}
            w.pw = []
        for w in pwrites:
            w.pw.append(o)
        if dma:
            o.signal = True
        self.ops[eng].append(o)
        return o

    def emit(self, final_wait_ops):
        nc = self.nc
        sems = {}

        def getsem(name):
            if name not in sems:
                sems[name] = nc.alloc_semaphore(name)
            return sems[name]

        allops = sorted((o for e in ENGS for o in self.ops[e]), key=lambda o: o.idx)
        cnt = {e: 0 for e in ENGS}
        epoch = {e: 0 for e in ENGS}
        dma_n = {e: 0 for e in ENGS}
        slot_last = {}
        slot_cnt = {}
        slot_ep = {}
        for o in allops:
            if o.is_dma:
                s = dma_n[o.eng] % self.dma_slots
                dma_n[o.eng] += 1
                key = (o.eng, s)
                o.slotprev = slot_last.get(key)
                c = slot_cnt.get(key, 0) + 16
                if c > self.sem_limit:
                    slot_ep[key] = slot_ep.get(key, 0) + 1
                    c = 16
                slot_cnt[key] = c
                slot_last[key] = o
                o.token = ("d_%s_%d_%d" % (o.eng, s, slot_ep.get(key, 0)), c)
            elif o.signal:
                cnt[o.eng] += 1
                if cnt[o.eng] > self.sem_limit:
                    epoch[o.eng] += 1
                    cnt[o.eng] = 1
                o.token = ("c_%s_%d" % (o.eng, epoch[o.eng]), cnt[o.eng])
        for o in allops:
            if o.token is not None:
                getsem(o.token[0])
        with nc.Block() as block:
            def gen(ename):
                def body(eng):
                    waited = {}
                    for o in self.ops[ename]:
                        need = {}
                        deps = o.deps
                        if o.slotprev is not None:
                            deps = deps + [o.slotprev]
                        for d in deps:
                            sname, val = d.token
                            if waited.get(sname, 0) >= val:
                                continue
                            if need.get(sname, 0) < val:
                                need[sname] = val
                        for sname, val in need.items():
                            eng.wait_ge(sems[sname], val)
                            waited[sname] = val
                        ins = o.fn(eng)
                        if o.token is not None:
                            ins.then_inc(sems[o.token[0]], 16 if o.is_dma else 1)
                    for d in final_wait_ops.get(ename, ()):
                        sname, val = d.token
                        if waited.get(sname, 0) < val:
                            eng.wait_ge(sems[sname], val)
                            waited[sname] = val
                return body

            block.tensor(gen("pe"))
            block.scalar(gen("act"))
            block.vector(gen("dve"))
            block.gpsimd(gen("pool"))
            block.sync(gen("sp"))


class Arena:
    def __init__(self, nc, nbytes):
        self.t = nc.alloc_sbuf_tensor("arena", [128, nbytes // 2], BF)
        self.nbytes = nbytes
        self.regions = []
        self.off = 0

    def reset(self, off=0):
        self.off = off

    def alloc(self, name, free_shape, dtype):
        esz = 4 if dtype == F32 else 2
        n = 1
        for s in free_shape:
            n *= s
        nb = (n * esz + 63) // 64 * 64
        start = self.off
        end = start + nb
        assert end <= self.nbytes, "arena overflow %s %d > %d" % (name, end, self.nbytes)
        self.off = end
        v = self.t[:, start // 2: start // 2 + n * esz // 2]
        if dtype == F32:
            v = v.bitcast(F32)
        if len(free_shape) == 2:
            v = v.rearrange("p (a b) -> p a b", b=free_shape[1])
        elif len(free_shape) == 3:
            v = v.rearrange("p (a b c) -> p a b c", b=free_shape[1], c=free_shape[2])
        r = Res(name)
        for (s0, e0, r0) in self.regions:
            if s0 < end and start < e0:
                for k, o in r0.readers.items():
                    r.readers[("inh", id(o))] = o
                if r0.last_w is not None:
                    r.readers[("inh", id(r0.last_w))] = r0.last_w
        self.regions.append((start, end, r))
        return v, r

    def extra_res(self, base, names):
        out = []
        for (s0, e0, r0) in list(self.regions):
            if r0 is base:
                for nm in names:
                    r = Res(nm)
                    r.readers = dict(base.readers)
                    self.regions.append((s0, e0, r))
                    out.append(r)
        return out


class PieceStream:
    def __init__(self, b, ring, pieces):
        self.b = b
        self.ring = ring
        self.pieces = pieces
        self.i_load = 0
        self.i_get = 0

    def get(self):
        R = len(self.ring)
        n = self.i_get
        while self.i_load < len(self.pieces) and self.i_load <= n + R - 1:
            k = self.i_load
            wv, rw = self.ring[k % R]
            ap, res = self.pieces[k]
            self.b.dma("sp", wv, ap, [res], [rw])
            self.i_load += 1
        self.i_get += 1
        return self.ring[n % R]


def qtiles(h):
    w = 256 if h == 0 else 512
    return [(t, c, w) for t in range(NSEG) for c in range(0, SEG, w)]


def blocks_for(t, c, w):
    out = []
    for tp in range(t + 1):
        for kb in range(4):
            out.append((TOK + tp * SEG + kb * 128, 32 + tp * 4 + kb, 0, "oth", tp, kb))
    for tp in range(t):
        for kb in range(4):
            out.append((tp * SEG + kb * 128, tp * 4 + kb, 0, "own", tp, kb))
    for kb in range(4):
        if kb * 128 < c:
            out.append((t * SEG + kb * 128, t * 4 + kb, 0, "own", t, kb))
        elif kb * 128 < c + w:
            out.append((t * SEG + kb * 128, t * 4 + kb, kb * 128 - c, "diag", t, kb))
    return out


def dist_columns():
    cols = {}
    n = 0
    for w in (512, 256):
        for t in range(NSEG):
            for c in range(0, SEG, w):
                for i, _ in enumerate(blocks_for(t, c, w)):
                    cols[(w, t, c, i)] = n
                    n += 1
    return cols, n


def dist_table(j):
    cols, n = dist_columns()
    tab = np.zeros((128, n), np.float32)
    pos_own = [s * SEG for s in OWN[j]]
    pos_oth = [s * SEG for s in OWN[1 - j]]
    kl = np.arange(128, dtype=np.float32)
    for w in (512, 256):
        for t in range(NSEG):
            for c in range(0, SEG, w):
                qref = pos_own[t] + c + w // 2
                for i, (kcol, vblk, c0, kind, tp, kb) in enumerate(blocks_for(t, c, w)):
                    if kind == "oth":
                        if pos_oth[tp] > pos_own[t]:
                            tab[:, cols[(w, t, c, i)]] = NEG_BIG
                            continue
                        ks = pos_oth[tp] + kb * 128
                    else:
                        ks = pos_own[tp] + kb * 128
                    tab[:, cols[(w, t, c, i)]] = ks + kl - qref
    return tab


class B:
    def __init__(self, nc):
        self.nc = nc
        self.S = Sched(nc)
        self.pool_dmas = []

    def mm(self, out, lhsT, rhs, start, stop, reads, writes, skip=False):
        if skip:
            return self.S.op("pe", lambda e: e.matmul(out, lhsT=lhsT, rhs=rhs, start=start, stop=stop, skip_group_check=True),
                             reads, writes)
        return self.S.op("pe", lambda e: e.matmul(out, lhsT=lhsT, rhs=rhs, start=start, stop=stop), reads, writes)

    def tr(self, out, in_, ident, reads, writes):
        return self.S.op("pe", lambda e: e.transpose(out=out, in_=in_, identity=ident), reads, writes)

    def act(self, out, in_, func, reads, writes, bias=None, scale=None, accum_out=None):
        kw = {}
        if bias is not None:
            kw["bias"] = bias
        if scale is not None:
            kw["scale"] = scale
        if accum_out is not None:
            kw["accum_out"] = accum_out
        return self.S.op("act", lambda e: e.activation(out=out, in_=in_, func=func, **kw), reads, writes)

    def tsc(self, eng, out, in0, s1, s2, op0, op1, reads, writes):
        if op1 is None:
            return self.S.op(eng, lambda e: e.tensor_scalar(out=out, in0=in0, scalar1=s1, scalar2=None, op0=op0), reads, writes)
        return self.S.op(eng, lambda e: e.tensor_scalar(out=out, in0=in0, scalar1=s1, scalar2=s2, op0=op0, op1=op1), reads, writes)

    def tt(self, eng, out, in0, in1, op, reads, writes):
        return self.S.op(eng, lambda e: e.tensor_tensor(out=out, in0=in0, in1=in1, op=op), reads, writes)

    def stt(self, eng, out, in0, scalar, in1, op0, op1, reads, writes):
        return self.S.op(eng, lambda e: e.scalar_tensor_tensor(out=out, in0=in0, scalar=scalar, in1=in1, op0=op0, op1=op1), reads, writes)

    def cp(self, eng, out, in_, reads, writes):
        return self.S.op(eng, lambda e: e.tensor_copy(out=out, in_=in_), reads, writes)

    def red(self, eng, out, in_, reads, writes):
        return self.S.op(eng, lambda e: e.tensor_reduce(out=out, in_=in_, axis=AX.X, op=ALU.add), reads, writes)

    def recip(self, out, in_, reads, writes):
        return self.S.op("dve", lambda e: e.reciprocal(out=out, in_=in_), reads, writes)

    def memset(self, eng, ap, val, writes):
        return self.S.op(eng, lambda e: e.memset(ap, val), (), writes)

    def dma(self, q, out, in_, reads, writes, pwrites=(), slow=False):
        o = self._dma(q, out, in_, reads, writes, pwrites, slow)
        if q == "pool":
            self.pool_dmas.append(o)
        return o

    def _dma(self, q, out, in_, reads, writes, pwrites=(), slow=False):
        if slow:
            return self.S.op(q, lambda e: e.dma_start(out=out, in_=in_, allow_slow_non_contiguous=True), reads, writes,
                             dma=True, pwrites=pwrites)
        return self.S.op(q, lambda e: e.dma_start(out=out, in_=in_), reads, writes, dma=True, pwrites=pwrites)


def build_program():
    nc = bass.Bass("TRN2", target_bir_lowering=False)
    b = B(nc)
    S = b.S
    dk = "ExternalOutput" if DEBUG else "Internal"

    def din(name, shape):
        return nc.dram_tensor(name, list(shape), F32, kind="ExternalInput").ap()

    x_own = din("x_own", [TOK, D])
    x_oth = din("x_oth", [TOK, D])
    w_in = din("w_in", [D, INW])
    w_out = din("w_out", [D, D])
    w_up = din("w_up", [D, DFF])
    w_dn = din("w_dn", [DFF, D])
    g_pre_mix = din("g_pre_mix", [D])
    g_post_mix = din("g_post_mix", [D])
    g_pre_mlp = din("g_pre_mlp", [D])
    g_post_mlp = din("g_post_mlp", [D])
    ln_g = din("ln_g", [1024])
    ln_b = din("ln_b", [1024])
    w_s = din("w_s", [8, 128, 128])
    b_s = din("b_s", [1024])
    lam_in = din("lam_in", [4, 64])
    subln = din("subln", [128])
    cols, NDC = dist_columns()
    dist_in = din("dist", [128, NDC])
    out = nc.dram_tensor("out", [TOK, D], F32, kind="ExternalOutput").ap()

    win_bf = nc.dram_tensor("win_bf", [10, 128, 16, 512], BF).ap()
    wout_bf = nc.dram_tensor("wout_bf", [4, 128, 16, 512], BF).ap()
    wup_bf = nc.dram_tensor("wup_bf", [16, 128, 16, 512], BF).ap()
    wdn_bf = nc.dram_tensor("wdn_bf", [16, 128, 16, 512], BF).ap()
    q_scr = nc.dram_tensor("q_scr", [8, 128, TOK], BF, kind=dk).ap()
    k_scr = nc.dram_tensor("k_scr", [8, 128, S_FULL], BF, kind=dk).ap()
    v_scr = nc.dram_tensor("v_scr", [8, 128, 64, 128], BF, kind=dk).ap()
    y_scr = nc.dram_tensor("y_scr", [16, 128, TOK], BF, kind=dk).ap()

    R_win = [Res("win%d" % i) for i in range(10)]
    R_wout = [Res("wout%d" % i) for i in range(4)]
    R_wup = [Res("wup%d" % i) for i in range(16)]
    R_wdn = [Res("wdn%d" % i) for i in range(16)]
    R_q = [Res("qscr%d" % t) for t in range(8)]
    R_k = [Res("kscr%d" % t) for t in range(16)]
    R_v = [Res("vscr%d" % t) for t in range(16)]
    R_ya = [Res("ya%d" % t) for t in range(8)]
    R_yb = [[Res("yb%d_%d" % (h, t)) for t in range(8)] for h in range(8)]
    R_out = [[Res("out%d_%d" % (t, i)) for i in range(4)] for t in range(8)]

    psum_all = nc.alloc_psum_tensor("psum_all", [128, 8 * 512], F32)
    banks = [psum_all[:, i * 512:(i + 1) * 512] for i in range(8)]
    R_bank = [Res("bank%d" % i) for i in range(8)]

    def bank_bf(i):
        return banks[i].bitcast(BF)

    ident = nc.alloc_sbuf_tensor("ident", [128, 128], BF)
    ones_bf = nc.alloc_sbuf_tensor("ones_bf", [128, 128], BF)
    ones_f = nc.alloc_sbuf_tensor("ones_f", [128, 128], F32)
    tri = nc.alloc_sbuf_tensor("tri", [128, 128], BF)
    ctmp = nc.alloc_sbuf_tensor("ctmp", [128, 128], F32)
    gcol_in = nc.alloc_sbuf_tensor("gcol_in", [128, 16], F32)
    gcol_up = nc.alloc_sbuf_tensor("gcol_up", [128, 16], F32)
    gcol_sub = nc.alloc_sbuf_tensor("gcol_sub", [128, 1], F32)
    ones_col = nc.alloc_sbuf_tensor("ones_col", [128, 1], F32)
    eps_col = nc.alloc_sbuf_tensor("eps_col", [128, 1], F32)
    neg_lam = nc.alloc_sbuf_tensor("neg_lam", [128, 1], F32)
    lamt = nc.alloc_sbuf_tensor("lamt", [128, 4, 64], F32)
    lamp = nc.alloc_sbuf_tensor("lamp", [128, 2, 64], F32)
    lams = nc.alloc_sbuf_tensor("lams", [128, 2], F32)
    R_c = {n: Res(n) for n in "ident ones tri ctmp gcol_in gcol_up gcol_sub ones_col eps_col neg_lam lamt lamp lams".split()}

    b.memset("pool", ctmp[:], 1.0, [R_c["ctmp"]])
    S.op("pool", lambda e: e.affine_select(out=ctmp[:], in_=ctmp[:], pattern=[[1, 128]], compare_op=ALU.is_equal,
                                           fill=0.0, base=0, channel_multiplier=-1), [R_c["ctmp"]], [R_c["ctmp"]])
    b.cp("dve", ident[:], ctmp[:], [R_c["ctmp"]], [R_c["ident"]])
    b.memset("pool", ctmp[:], 1.0, [R_c["ctmp"]])
    S.op("pool", lambda e: e.affine_select(out=ctmp[:], in_=ctmp[:], pattern=[[1, 128]], compare_op=ALU.is_ge,
                                           fill=0.0, base=0, channel_multiplier=-1), [R_c["ctmp"]], [R_c["ctmp"]])
    b.cp("dve", tri[:], ctmp[:], [R_c["ctmp"]], [R_c["tri"]])
    b.memset("pool", ones_bf[:], 1.0, [R_c["ones"]])
    b.memset("pool", ones_f[:], 1.0, [R_c["ones"]])
    b.memset("pool", ones_col[:], 1.0, [R_c["ones_col"]])
    b.memset("pool", eps_col[:], EPS, [R_c["eps_col"]])
    b.dma("sp", gcol_in[:], g_pre_mix.rearrange("(c p) -> p c", p=128), [], [R_c["gcol_in"]], slow=True)
    b.dma("sp", gcol_up[:], g_pre_mlp.rearrange("(c p) -> p c", p=128), [], [R_c["gcol_up"]], slow=True)
    b.dma("sp", gcol_sub[:], subln.rearrange("(p o) -> p o", o=1), [], [R_c["gcol_sub"]], slow=True)
    b.tsc("dve", gcol_sub[:], gcol_sub[:], 1.0 - LAMBDA_INIT, None, ALU.mult, None, [R_c["gcol_sub"]], [R_c["gcol_sub"]])
    b.dma("sp", lamt[:], lam_in.partition_broadcast(128), [], [R_c["lamt"]])
    b.tt("dve", lamp[:, 0, :], lamt[:, 0, :], lamt[:, 1, :], ALU.mult, [R_c["lamt"]], [R_c["lamp"]])
    b.tt("dve", lamp[:, 1, :], lamt[:, 2, :], lamt[:, 3, :], ALU.mult, [R_c["lamt"]], [R_c["lamp"]])
    b.red("dve", lams[:], lamp[:], [R_c["lamp"]], [R_c["lams"]])
    b.act(lams[:], lams[:], AF.Exp, [R_c["lams"]], [R_c["lams"]])
    b.tsc("dve", neg_lam[:], lams[:, 1:2], -LAMBDA_INIT, None, ALU.add, None, [R_c["lams"]], [R_c["neg_lam"]])
    b.tt("dve", neg_lam[:], neg_lam[:], lams[:, 0:1], ALU.subtract, [R_c["lams"], R_c["neg_lam"]], [R_c["neg_lam"]])

    ar = Arena(nc, 200 * 1024)

    cast_units = []

    def add_units_rowmajor(src, dst, Rdst, ncols, colfn, rows_chunks, piece_of):
        for rc in range(rows_chunks):
            for u in range(ncols // 1024):
                s_ap = src[rc * 128:(rc + 1) * 128, u * 1024:(u + 1) * 1024]
                pieces = piece_of(rc, u)
                cast_units.append((s_ap, colfn(rc), [(dst[p, :, kc, :], Rdst[p]) for (p, kc) in pieces]))

    add_units_rowmajor(w_in, win_bf, R_win, INW, lambda rc: gcol_in[:, rc:rc + 1], 16,
                       lambda rc, u: [(2 * u, rc), (2 * u + 1, rc)])
    n_win_units = len(cast_units)
    add_units_rowmajor(w_out, wout_bf, R_wout, D, lambda rc: (None if rc < 8 else gcol_sub[:]), 16,
                       lambda rc, u: [(2 * u, rc), (2 * u + 1, rc)])
    add_units_rowmajor(w_up, wup_bf, R_wup, DFF, lambda rc: gcol_up[:, rc:rc + 1], 16,
                       lambda rc, u: [(2 * u, rc), (2 * u + 1, rc)])
    add_units_rowmajor(w_dn, wdn_bf, R_wdn, D, lambda rc: None, 64,
                       lambda rc, u: [((rc // 16) * 4 + 2 * u, rc % 16), ((rc // 16) * 4 + 2 * u + 1, rc % 16)])

    cast_state = {"i": 0}
    CS = 2
    cst_f = []
    cst_b = []

    def alloc_cast_staging():
        cst_f.clear()
        cst_b.clear()
        for i in range(CS):
            cst_f.append(ar.alloc("cst_f%d" % i, [1024], F32))
            cst_b.append(ar.alloc("cst_b%d" % i, [2, 512], BF))

    def do_casts(n, eng="dve"):
        for _ in range(n):
            i = cast_state["i"]
            if i >= len(cast_units):
                return
            cast_state["i"] = i + 1
            s_ap, col, dsts = cast_units[i]
            (f, rf) = cst_f[i % CS]
            (bt, rb) = cst_b[i % CS]
            b.dma("sp", f, s_ap, [], [rf])
            if col is None:
                b.cp("pool", bt.rearrange("p a b -> p (a b)"), f, [rf], [rb])
            elif eng == "act":
                b.act(bt.rearrange("p a b -> p (a b)"), f, AF.Copy, [rf, R_c["gcol_in"], R_c["gcol_up"], R_c["gcol_sub"], R_c["ones_col"]], [rb], scale=col)
            else:
                b.tsc(eng, bt.rearrange("p a b -> p (a b)"), f, col, None, ALU.mult, None,
                      [rf, R_c["gcol_in"], R_c["gcol_up"], R_c["gcol_sub"], R_c["ones_col"]], [rb])
            for k, (d_ap, d_res) in enumerate(dsts):
                b.dma("pool", d_ap, bt[:, k, :], [rb], [], pwrites=[d_res])

    ar.reset(0)
    alloc_cast_staging()
    cast_end = ar.off
    wmT = ar.alloc("wmT", [8, 128], BF)
    _off0 = ar.off
    wmf = ar.alloc("wmf", [8, 128], F32)
    wsb16 = ar.alloc("wsb16", [8, 128], BF)
    b.dma("sp", wmf[0], w_s.rearrange("g t s -> t g s"), [], [wmf[1]])
    b.cp("dve", wsb16[0], wmf[0], [wmf[1]], [wsb16[1]])
    for half in range(2):
        for g4 in range(4):
            g = half * 4 + g4
            b.tr(bank_bf(0)[:, g4 * 128:(g4 + 1) * 128], wsb16[0][:, g, :], ident[:], [wsb16[1], R_c["ident"]], [R_bank[0]])
        b.cp("dve", wmT[0][:, half * 4:half * 4 + 4, :], bank_bf(0)[:, 0:512].rearrange("p (g t) -> p g t", t=128),
             [R_bank[0]], [wmT[1]])
    for g in range(8):
        b.tt("pool", wmT[0][:, g, :], wmT[0][:, g, :], tri[:], ALU.mult, [wmT[1], R_c["tri"]], [wmT[1]])
    ar.reset(_off0)
    xin = [ar.alloc("xin%d" % i, [D], F32) for i in range(3)]
    junk = ar.alloc("junk", [D], BF)
    xs = [ar.alloc("xs%d" % i, [D], BF) for i in range(2)]
    xsT = []
    for s_ in range(2):
        v_, r_ = ar.alloc("xsT%d" % s_, [16, 512], BF)
        xsT.append((v_, ar.extra_res(r_, ["xsT%d_t%d" % (s_, t_) for t_ in range(4)])))
    wring = [ar.alloc("wring%d" % i, [16, 512], BF) for i in range(3)]
    uT = ar.alloc("uT", [8, 512], BF)
    vg = ar.alloc("vg", [4, 1024], F32)
    sqt = ar.alloc("sqt", [1024], F32)
    vln = [ar.alloc("vln%d" % i, [8, 128], BF) for i in range(4)]
    ctm = ar.alloc("ctm", [1024], F32)
    stg = [ar.alloc("stg%d" % i, [2048], BF) for i in range(4)]
    stg_ctr = {"i": 0}

    def next_stg():
        v_, r_ = stg[stg_ctr["i"] % 4]
        stg_ctr["i"] += 1
        return v_, r_
    bsb = ar.alloc("bsb", [1024], F32)
    lng = ar.alloc("lng", [1024], F32)
    lnb = ar.alloc("lnb", [1024], F32)
    ssA = [ar.alloc("ssA%d" % i, [4], F32) for i in range(2)]
    st1 = ar.alloc("st1", [32], F32)
    st2 = ar.alloc("st2", [32], F32)
    st3 = ar.alloc("st3", [32], F32)
    print("phase A arena bytes", ar.off)

    b.dma("sp", bsb[0], b_s.partition_broadcast(128), [], [bsb[1]])
    b.dma("sp", lng[0], ln_g.partition_broadcast(128), [], [lng[1]])
    b.dma("sp", lnb[0], ln_b.partition_broadcast(128), [], [lnb[1]])

    do_casts(n_win_units, eng="dve")
    CASTS_A = 6
    rest_per_iter = CASTS_A

    seglist = []
    for t in range(NSEG):
        seglist.append(("own", t))
        seglist.append(("oth", t))

    def seg_x(kind, t, tile):
        src = x_own if kind == "own" else x_oth
        r0 = t * SEG + tile * 128
        return src[r0:r0 + 128, :]

    xin_ctr = {"i": 0}

    xslots = {}

    def xL(U):
        i_, u_ = U // 8, U % 8
        if i_ >= N_SEG_A or U in xslots:
            return
        kind_, t_ = seglist[i_]
        xi, rxi = xin[xin_ctr["i"] % 3]
        xin_ctr["i"] += 1
        b.dma("act", xi, seg_x(kind_, t_, u_ % 4), [], [rxi])
        xslots[U] = (xi, rxi)

    def norm_items(i):
        kind, t = seglist[i]
        sl = i % 2
        ss, rss = ssA[sl]
        slots = xslots

        def useA(u_loc):
            u = 8 * i + u_loc
            xL(u)
            xL(u + 1)
            xL(u + 2)
            xi, rxi = slots[u]
            tile = u_loc % 4
            if u_loc < 4:
                b.act(junk[0], xi, AF.Square, [rxi], [junk[1], rss], accum_out=ss[:, tile:tile + 1])
                if u_loc == 3:
                    b.tsc("dve", ss, ss, 1.0 / D, EPS, ALU.mult, ALU.add, [rss], [rss])
                    b.act(ss, ss, AF.Sqrt, [rss], [rss])
                    b.recip(ss, ss, [rss], [rss])
                return
            xsb, rxs = xs[tile % 2]
            b.act(xsb, xi, AF.Copy, [rxi, rss], [rxs], scale=ss[:, tile:tile + 1])

        def useB(u_loc):
            tile = u_loc % 4
            xsb, rxs = xs[tile % 2]
            xtv, xtr = xsT[sl]
            for half in range(2):
                for c8 in range(8):
                    kc = half * 8 + c8
                    b.tr(bank_bf(half)[:, c8 * 128:(c8 + 1) * 128], xsb[:, kc * 128:(kc + 1) * 128], ident[:],
                         [rxs, R_c["ident"]], [R_bank[half]])
                dst = xtv[:, half * 8:half * 8 + 8, tile * 128:(tile + 1) * 128]
                src = bank_bf(half).rearrange("p (c t) -> p c t", t=128)
                b.act(dst, src, AF.Copy, [R_bank[half]], [xtr[tile]])

        items = []
        for u in range(4):
            items.append(lambda u=u: useA(u))
        items.append(lambda: useA(4))
        for u in range(4, 7):
            items.append(lambda u=u: (useB(u), useA(u + 1)))
        items.append(lambda: useB(7))
        return items

    pumpq = []

    def pump(n=1):
        for _ in range(n):
            if pumpq:
                pumpq.pop(0)()

    pieces_A = []
    for i_ in range(N_SEG_A):
        kind_, _t = seglist[i_]
        for pc_ in (range(10) if kind_ == "own" else range(6, 10)):
            pieces_A.append((win_bf[pc_], R_win[pc_]))
    stream = {"s": PieceStream(b, wring, pieces_A)}

    def load_piece_p(npump):
        pump(npump)
        return stream["s"].get()

    fb_ctr = {"i": 0}

    def next_fbank():
        i = 2 + (fb_ctr["i"] % 2)
        fb_ctr["i"] += 1
        return i

    tb_ctr = {"i": 0}

    def next_tbank():
        i = 4 + (tb_ctr["i"] % 4)
        tb_ctr["i"] += 1
        return i

    def fm_block(sl, wv, rw, cb, evac):
        bi = next_fbank()
        xtv, xtr = xsT[sl]
        for kc in range(16):
            b.mm(banks[bi][:, :], wv[:, kc, cb * 128:(cb + 1) * 128], xtv[:, kc, :],
                 kc == 0, kc == 15, [rw] + xtr, [R_bank[bi]])
        evac(bi)

    def tm_block(sl, wv, rw, tile, evac):
        bi = next_tbank()
        xtv, xtr = xsT[sl]
        for kc in range(16):
            b.mm(banks[bi][:, :], xtv[:, kc, tile * 128:(tile + 1) * 128], wv[:, kc, :], kc == 0, kc == 15,
                 [rw, xtr[tile]], [R_bank[bi]])
        evac(bi)

    def seg_compute(i):
        kind, t = seglist[i]
        sl = i % 2
        slot = t if kind == "own" else 8 + t
        NP = 2 if kind == "own" else 4
        if kind == "own":
            for pc in range(2):
                wv, rw = load_piece_p(NP)
                for cb in range(4):
                    g = pc * 4 + cb
                    fm_block(sl, wv, rw, cb, lambda bi, g=g: b.act(uT[0][:, g, :], banks[bi][:, :], AF.Gelu_apprx_tanh,
                                                                  [R_bank[bi]], [uT[1]]))
            for pc in range(2):
                wv, rw = load_piece_p(NP)
                for tile in range(4):
                    tm_block(sl, wv, rw, tile, lambda bi, tile=tile, pc=pc: b.act(
                        vg[0][:, tile, pc * 512:(pc + 1) * 512], banks[bi][:, :], AF.Gelu_apprx_tanh, [R_bank[bi]], [vg[1]]))
            vg3 = vg[0].rearrange("p a (g d) -> p (a g) d", d=128)
            b.red("dve", st1[0], vg3, [vg[1]], [st1[1]])
            for tile in range(4):
                b.tt("dve", sqt[0], vg[0][:, tile, :], vg[0][:, tile, :], ALU.mult, [vg[1]], [sqt[1]])
                b.red("dve", st2[0][:, tile * 8:(tile + 1) * 8], sqt[0].rearrange("p (g d) -> p g d", d=128), [sqt[1]], [st2[1]])
            b.tsc("dve", st1[0], st1[0], 1.0 / 128, None, ALU.mult, None, [st1[1]], [st1[1]])
            b.tt("dve", st3[0], st1[0], st1[0], ALU.mult, [st1[1]], [st3[1]])
            b.stt("dve", st2[0], st2[0], 1.0 / 128, st3[0], ALU.mult, ALU.subtract, [st2[1], st3[1]], [st2[1]])
            b.tsc("dve", st2[0], st2[0], EPS, None, ALU.add, None, [st2[1]], [st2[1]])
            b.act(st2[0], st2[0], AF.Sqrt, [st2[1]], [st2[1]])
            b.recip(st2[0], st2[0], [st2[1]], [st2[1]])
            for tile in range(4):
                vt = vg[0][:, tile, :].rearrange("p (g d) -> p g d", d=128)
                mb = st1[0][:, tile * 8:(tile + 1) * 8].unsqueeze(2).to_broadcast([128, 8, 128])
                rb_ = st2[0][:, tile * 8:(tile + 1) * 8].unsqueeze(2).to_broadcast([128, 8, 128])
                b.tt("dve", vt, vt, mb, ALU.subtract, [vg[1], st1[1]], [vg[1]])
                b.tt("dve", vt, vt, rb_, ALU.mult, [vg[1], st2[1]], [vg[1]])
                b.tt("dve", vg[0][:, tile, :], vg[0][:, tile, :], lng[0], ALU.mult, [vg[1], lng[1]], [vg[1]])
                vl, rvl = vln[tile]
                b.tt("dve", vl.rearrange("p g d -> p (g d)"), vg[0][:, tile, :], lnb[0], ALU.add, [vg[1], lnb[1]], [rvl])
            for pc in range(2):
                wv, rw = load_piece_p(NP)
                sv, sr = next_stg()
                sv3 = sv.rearrange("p (a b) -> p a b", b=512)
                for cb in range(4):
                    fm_block(sl, wv, rw, cb, lambda bi, cb=cb, sv3=sv3, sr=sr: b.act(sv3[:, cb, :], banks[bi][:, :], AF.Copy, [R_bank[bi]], [sr]))
                b.dma("pool", q_scr[pc * 4:(pc + 1) * 4, :, t * SEG:(t + 1) * SEG].rearrange("h p t -> p h t"), sv3, [sr], [],
                      pwrites=[R_q[t]])
        for pc in range(2):
            wv, rw = load_piece_p(NP)
            sv, sr = next_stg()
            sv3 = sv.rearrange("p (a b) -> p a b", b=512)
            for cb in range(4):
                fm_block(sl, wv, rw, cb, lambda bi, cb=cb, sv3=sv3, sr=sr: b.act(sv3[:, cb, :], banks[bi][:, :], AF.Copy, [R_bank[bi]], [sr]))
            b.dma("pool", k_scr[pc * 4:(pc + 1) * 4, :, slot * SEG:(slot + 1) * SEG].rearrange("h p t -> p h t"), sv3, [sr], [],
                  pwrites=[R_k[slot]])
        if kind == "own":
            for tile in range(4):
                vl, rvl = vln[tile]
                for half in range(2):
                    bi = next_fbank()
                    for g4 in range(4):
                        g = half * 4 + g4
                        b.mm(banks[bi][:, g4 * 128:(g4 + 1) * 128], vl[:, g, :], wmT[0][:, g, :], True, True,
                             [rvl, wmT[1]], [R_bank[bi]])
                    b.tt("dve", ctm[0][:, half * 512:(half + 1) * 512], banks[bi][:, :], bsb[0][:, half * 512:(half + 1) * 512],
                         ALU.add, [R_bank[bi], bsb[1]], [ctm[1]])
                if tile % 2 == 0:
                    ysv, ysr = next_stg()
                    ysv3 = ysv.rearrange("p (a b) -> p a b", b=256)
                b.tt("dve", ysv3[:, :, (tile % 2) * 128:(tile % 2 + 1) * 128], ctm[0].rearrange("p (g t) -> p g t", t=128),
                     uT[0][:, :, tile * 128:(tile + 1) * 128], ALU.mult, [ctm[1], uT[1]], [ysr])
                if tile % 2 == 1:
                    c_lo = t * SEG + (tile - 1) * 128
                    b.dma("pool", y_scr[0:8, :, c_lo:c_lo + 256].rearrange("c p t -> p c t"), ysv3, [ysr], [], pwrites=[R_ya[t]])
        for pc in range(2):
            wv, rw = load_piece_p(NP)
            sv, sr = next_stg()
            sv3 = sv.rearrange("p (a b) -> p a b", b=512)
            for tile in range(4):
                tm_block(sl, wv, rw, tile, lambda bi, tile=tile, sv3=sv3, sr=sr: b.act(
                    sv3[:, tile, :], banks[bi][:, :], AF.Copy, [R_bank[bi]], [sr]))
            for tile in range(4):
                b.dma("pool", v_scr[pc * 4:(pc + 1) * 4, :, slot * 4 + tile, :].rearrange("h p d -> p h d"),
                      sv3[:, tile, :].rearrange("p (h d) -> p h d", d=128), [sr], [], pwrites=[R_v[slot]])

    for it_ in norm_items(0):
        it_()
    for i in range(N_SEG_A):
        if i + 1 < N_SEG_A:
            pumpq.extend(norm_items(i + 1))
        ncast = rest_per_iter if PHASE_LIMIT != "A" else 0
        per = (ncast + 3) // 4
        for _ in range(4):
            pumpq.append(lambda per=per: do_casts(per, eng="dve"))
        seg_compute(i)
        pump(100)
    if PHASE_LIMIT == "A":
        S.emit({"pool": b.pool_dmas[-16:]})
        print("ops:", {e: len(S.ops[e]) for e in ENGS})
        return nc
    ar.reset(cast_end)
    kTb = [ar.alloc("kT%d" % i, [S_FULL], BF) for i in range(2)]
    Vb = [ar.alloc("V%d" % i, [64, 128], BF) for i in range(2)]
    qTb = [ar.alloc("qT%d" % i, [TOK], BF) for i in range(2)]
    Pb = [ar.alloc("P%d" % i, [2, 512], BF) for i in range(4)]
    e_r = [ar.alloc("e_r%d" % i, [512], F32) for i in range(2)]
    e_t = [ar.alloc("e_t%d" % i, [512], F32) for i in range(2)]
    e_o = ar.alloc("e_o", [512], F32)
    e_sq = ar.alloc("e_sq", [512], BF)
    e_rs = ar.alloc("e_rs", [512], F32)
    ybst = [ar.alloc("ybst%d" % i, [512], BF) for i in range(2)]
    Lacc = []
    for i_ in range(2):
        v_, r_ = ar.alloc("Lacc%d" % i_, [2, 512], F32)
        Lacc.append((v_, ar.extra_res(r_, ["Lacc%d_m%d" % (i_, m_) for m_ in range(2)])))
    e_ln = ar.alloc("e_ln", [512], F32)
    distt = ar.alloc("distt", [NDC], F32)
    biasT = ar.alloc("biasT", [8, 288], F32)
    bias0 = ar.alloc("bias0", [NDC - 288], F32)
    print("phase B arena bytes", ar.off)

    b.dma("sp", distt[0], dist_in, [], [distt[1]])
    for h in range(1, 8):
        b.tsc("dve", biasT[0][:, h, :], distt[0][:, 0:288], SLOPES[h], None, ALU.mult, None, [distt[1]], [biasT[1]])
    b.tsc("dve", bias0[0], distt[0][:, 288:NDC], SLOPES[0], None, ALU.mult, None, [distt[1]], [bias0[1]])

    SB = [(0, 1), (2, 3)]
    OBP = [(4, 5), (6, 7)]
    yb_ctr = {"i": 0}
    n_qt_total = sum(len(qtiles(h)) for h in range(8))
    casts_left = max(0, len(cast_units) - cast_state["i"])
    casts_per_qt = (casts_left + n_qt_total - 1) // n_qt_total

    def load_head(h):
        b.dma("sp", kTb[h % 2][0], k_scr[h], R_k, [kTb[h % 2][1]])
        b.dma("sp", Vb[h % 2][0], v_scr[h], R_v, [Vb[h % 2][1]])
        b.dma("sp", qTb[h % 2][0], q_scr[h], R_q, [qTb[h % 2][1]])

    tasks = []
    T_ = -1
    for h in range(8):
        for (t, c, w) in qtiles(h):
            T_ += 1
            blks = blocks_for(t, c, w)
            for i, blk in enumerate(blks):
                tasks.append((h, t, c, w, i, len(blks), blk, T_))

    def emit_qk(n):
        h, t, c, w, i, nb_, (kcol, vblk, c0, kind, tp, kb), T = tasks[n]
        kT, rkT = kTb[h % 2]
        qT, rqT = qTb[h % 2]
        sb = SB[n % 2]
        q0 = t * SEG + c
        nn = w - c0
        for m in range(2):
            b.mm(banks[sb[m]][:, 0:nn], kT[m * 64:(m + 1) * 64, kcol:kcol + 128],
                 qT[m * 64:(m + 1) * 64, q0 + c0:q0 + w], True, True, [rkT, rqT], [R_bank[sb[m]]])

    deferred = []

    OB = (4, 5)
    LB = (6, 7)
    e_l = [ar.alloc("e_l%d" % i, [512], F32) for i in range(2)]

    def epilogue0(n):
        h, t, c, w, i, nb_, blk, T = tasks[n]
        b.cp("dve", e_t[0][0][:, 0:w], banks[OB[0]][:, 0:w], [R_bank[OB[0]]], [e_t[0][1]])
        b.act(e_t[1][0][:, 0:w], banks[OB[1]][:, 0:w], AF.Copy, [R_bank[OB[1]]], [e_t[1][1]])
        b.cp("dve", e_l[0][0][:, 0:w], banks[LB[0]][:, 0:w], [R_bank[LB[0]]], [e_l[0][1]])
        b.act(e_l[1][0][:, 0:w], banks[LB[1]][:, 0:w], AF.Copy, [R_bank[LB[1]]], [e_l[1][1]])

    def epilogue1(n):
        h, t, c, w, i, nb_, blk, T = tasks[n]
        for m in (0, 1):
            b.recip(e_r[m][0][:, 0:w], e_l[m][0][:, 0:w], [e_l[m][1]], [e_r[m][1]])
        for m in (0, 1):
            b.tt("dve", e_t[m][0][:, 0:w], e_t[m][0][:, 0:w], e_r[m][0][:, 0:w], ALU.mult, [e_t[m][1], e_r[m][1]], [e_t[m][1]])
        b.stt("dve", e_o[0][:, 0:w], e_t[1][0][:, 0:w], neg_lam[:], e_t[0][0][:, 0:w], ALU.mult, ALU.add,
              [e_t[0][1], e_t[1][1], R_c["neg_lam"]], [e_o[1]])
        b.tt("dve", e_sq[0][:, 0:w], e_o[0][:, 0:w], e_o[0][:, 0:w], ALU.mult, [e_o[1]], [e_sq[1]])

    ssb = {"i": 0}

    def epilogue2(n):
        h, t, c, w, i, nb_, blk, T = tasks[n]
        q0 = t * SEG + c
        sbk = SB[(ssb["n"] + 1) % 2][0] if False else None
        bi = SB[ssb["cur"] % 2][0]
        b.mm(banks[bi][:, 0:w], ones_bf[:], e_sq[0][:, 0:w], True, True, [R_c["ones"], e_sq[1]], [R_bank[bi]])
        b.act(e_ln[0][:, 0:w], banks[bi][:, 0:w], AF.Ln, [R_bank[bi], R_c["eps_col"]], [e_ln[1]], bias=eps_col[:], scale=1.0 / 128)
        b.act(e_rs[0][:, 0:w], e_ln[0][:, 0:w], AF.Exp, [e_ln[1]], [e_rs[1]], scale=-0.5)
        yb, ryb = ybst[yb_ctr["i"] % 2]
        yb_ctr["i"] += 1
        b.tt("dve", yb[:, 0:w], e_o[0][:, 0:w], e_rs[0][:, 0:w], ALU.mult, [e_o[1], e_rs[1]], [ryb])
        b.dma("pool", y_scr[8 + h, :, q0:q0 + w], yb[:, 0:w], [ryb], [], pwrites=[R_yb[h][t]])
        do_casts(casts_per_qt, eng="dve")

    def emit_rest(n):
        h, t, c, w, i, nb_, (kcol, vblk, c0, kind, tp, kb), T = tasks[n]
        Vt, rV = Vb[h % 2]
        sb = SB[n % 2]
        Pt, rP = Pb[n % 4]
        nn = w - c0
        if h == 0:
            cc = cols[(256, t, c, i)] - 288
            bcol = bias0[0][:, cc:cc + 1]
            rbias = bias0[1]
        else:
            cc = cols[(512, t, c, i)]
            bcol = biasT[0][:, h, cc:cc + 1]
            rbias = biasT[1]
        for m in range(2):
            b.act(Pt[:, m, 0:nn], banks[sb[m]][:, 0:nn], AF.Exp, [R_bank[sb[m]], rbias], [rP], bias=bcol, scale=QK_SCALE)
        if kind == "diag":
            b.tt("dve", Pt[:, :, 0:128], Pt[:, :, 0:128], tri[:].unsqueeze(1).to_broadcast([128, 2, 128]), ALU.mult,
                 [rP, R_c["tri"]], [rP])
        for m in range(2):
            b.mm(banks[OB[m]][:, c0:w], Vt[:, vblk, :], Pt[:, m, 0:nn], i == 0, i == nb_ - 1, [rV, rP], [R_bank[OB[m]]])
            b.mm(banks[LB[m]][:, c0:w], ones_bf[:], Pt[:, m, 0:nn], i == 0, i == nb_ - 1, [R_c["ones"], rP], [R_bank[LB[m]]])
        if i == nb_ - 1:
            epilogue0(n)
            deferred.append((n + 3, lambda n=n: epilogue1(n)))
            deferred.append((n + 6, lambda n=n: epilogue2(n)))
        ssb["cur"] = n
        deferred.sort(key=lambda x: x[0])
        while deferred and deferred[0][0] <= n:
            deferred.pop(0)[1]()

    load_head(0)
    NT_ = len(tasks)
    emit_qk(0)
    for n in range(NT_):
        h = tasks[n][0]
        if tasks[n][4] == 0 and tasks[n][1] == 0 and tasks[n][2] == 0 and h + 1 < 8:
            load_head(h + 1)
        if n + 1 < NT_:
            emit_qk(n + 1)
        emit_rest(n)
    while deferred:
        deferred.pop(0)[1]()
    do_casts(10 ** 6, eng="dve")

    if PHASE_LIMIT == "AB":
        S.emit({"pool": b.pool_dmas[-16:]})
        print("ops:", {e: len(S.ops[e]) for e in ENGS})
        return nc
    ar.reset(0)
    wring = [ar.alloc("wringC%d" % i, [16, 512], BF) for i in range(3)]
    pieces_C = []
    for t_ in range(NSEG):
        for nb_ in range(4):
            pieces_C.append((wout_bf[nb_], R_wout[nb_]))
        for part_ in range(4):
            for pc_ in range(4):
                pieces_C.append((wup_bf[part_ * 4 + pc_], R_wup[part_ * 4 + pc_]))
            for nb_ in range(4):
                pieces_C.append((wdn_bf[part_ * 4 + nb_], R_wdn[part_ * 4 + nb_]))
    stream["s"] = PieceStream(b, wring, pieces_C)
    yT = ar.alloc("yT", [16, 512], BF)
    _yv, _yr = ar.alloc("yacc", [4, D], F32)
    yacc = (_yv, None)
    yaccR = ar.extra_res(_yr, ["yacc_t%d" % i for i in range(4)])
    xin = [ar.alloc("xinC%d" % i, [D], F32) for i in range(2)]
    junk = ar.alloc("junkC", [D], BF)
    hn = [ar.alloc("hn%d" % i, [D], BF) for i in range(2)]
    hnT = ar.alloc("hnT", [16, 512], BF)
    f1T = [ar.alloc("f1T%d" % i, [16, 512], BF) for i in range(2)]
    rtmp = [ar.alloc("rtmp%d" % i, [512], F32) for i in range(2)]
    gpost = ar.alloc("gpost", [D], F32)
    gmlp = ar.alloc("gmlp", [D], F32)
    ssq = ar.alloc("ssq", [16], F32)
    rs1 = ar.alloc("rs1", [4], F32)
    rs2 = [ar.alloc("rs2_%d" % i, [1], F32) for i in range(4)]
    rs3 = ar.alloc("rs3", [4], F32)
    print("phase C arena bytes", ar.off)
    b.dma("sp", gpost[0], g_post_mix.partition_broadcast(128), [], [gpost[1]])
    b.dma("sp", gmlp[0], g_post_mlp.partition_broadcast(128), [], [gmlp[1]])

    ub_ctr = {"i": 0}
    ob_ctr = {"i": 0}
    final_dmas = []

    def next_obank():
        i = 4 + ob_ctr["i"] % 3
        ob_ctr["i"] += 1
        return i

    def rstd_from(ssv, rss_):
        b.tsc("dve", ssv, ssv, 1.0 / D, EPS, ALU.mult, ALU.add, [rss_], [rss_])
        b.act(ssv, ssv, AF.Sqrt, [rss_], [rss_])
        b.recip(ssv, ssv, [rss_], [rss_])

    def load_yT(t):
        yres = [R_ya[t]] + [R_yb[h][t] for h in range(8)]
        b.dma("sp", yT[0], y_scr[:, :, t * SEG:(t + 1) * SEG].rearrange("c p t -> p c t"), yres, [yT[1]])

    load_yT(0)
    for t in range(NSEG):
        for nb in range(4):
            wv, rw = stream["s"].get()
            for tile in range(4):
                bi = next_obank()
                for kc in range(16):
                    b.mm(banks[bi][:, :], yT[0][:, kc, tile * 128:(tile + 1) * 128], wv[:, kc, :], kc == 0, kc == 15,
                         [yT[1], rw], [R_bank[bi]])
                b.cp("dve", yacc[0][:, tile, nb * 512:(nb + 1) * 512], banks[bi][:, :], [R_bank[bi]], [yaccR[tile]])
                b.act(junk[0][:, 0:512], yacc[0][:, tile, nb * 512:(nb + 1) * 512], AF.Square, [yaccR[tile]], [ssq[1]],
                      accum_out=ssq[0][:, tile * 4 + nb:tile * 4 + nb + 1])
        if t + 1 < NSEG:
            load_yT(t + 1)
        b.red("dve", rs1[0], ssq[0].rearrange("p (a b) -> p a b", b=4), [ssq[1]], [rs1[1]])
        rstd_from(rs1[0], rs1[1])
        xsl = {}

        def xl(tile, t=t):
            xi, rxi = xin[tile % 2]
            r0 = t * SEG + tile * 128
            b.dma("sp", xi, x_own[r0:r0 + 128, :], [], [rxi])
            xsl[tile] = (xi, rxi)

        xl(0)
        for tile in range(4):
            if tile + 1 < 4:
                xl(tile + 1)
            xi, rxi = xsl[tile]
            r0 = t * SEG + tile * 128
            b.stt("dve", yacc[0][:, tile, :], yacc[0][:, tile, :], rs1[0][:, tile:tile + 1], gpost[0], ALU.mult, ALU.mult,
                  [yaccR[tile], rs1[1], gpost[1]], [yaccR[tile]])
            b.tt("pool", yacc[0][:, tile, :], yacc[0][:, tile, :], xi, ALU.add, [yaccR[tile], rxi], [yaccR[tile]])
            b.dma("pool", out[r0:r0 + 128, :], yacc[0][:, tile, :], [yaccR[tile]], [R_out[t][tile]])
            r2c, rr2 = rs2[tile]
            b.act(junk[0], yacc[0][:, tile, :], AF.Square, [yaccR[tile]], [rr2], accum_out=r2c)
            b.tsc("dve", r2c, r2c, 1.0 / D, EPS, ALU.mult, ALU.add, [rr2], [rr2])
            b.act(r2c, r2c, AF.Sqrt, [rr2], [rr2])
            b.recip(r2c, r2c, [rr2], [rr2])
            hb, rhb = hn[tile % 2]
            b.act(hb, yacc[0][:, tile, :], AF.Copy, [yaccR[tile], rr2], [rhb], scale=r2c)
            for half in range(2):
                for c8 in range(8):
                    kc = half * 8 + c8
                    b.tr(bank_bf(half)[:, c8 * 128:(c8 + 1) * 128], hb[:, kc * 128:(kc + 1) * 128], ident[:],
                         [rhb, R_c["ident"]], [R_bank[half]])
                if half == 0:
                    b.cp("dve", hnT[0][:, 0:8, tile * 128:(tile + 1) * 128], bank_bf(0).rearrange("p (c t) -> p c t", t=128),
                         [R_bank[0]], [hnT[1]])
                else:
                    b.act(hnT[0][:, 8:16, tile * 128:(tile + 1) * 128], bank_bf(1).rearrange("p (c t) -> p c t", t=128), AF.Copy,
                          [R_bank[1]], [hnT[1]])
        for part in range(4):
            ft, rft = f1T[part % 2]
            for pc in range(4):
                wv, rw = stream["s"].get()
                for cb in range(4):
                    bi = 2 + ub_ctr["i"] % 2
                    ub_ctr["i"] += 1
                    for kc in range(16):
                        b.mm(banks[bi][:, :], wv[:, kc, cb * 128:(cb + 1) * 128], hnT[0][:, kc, :], kc == 0, kc == 15,
                             [rw, hnT[1]], [R_bank[bi]])
                    rt, rrt = rtmp[ub_ctr["i"] % 2]
                    b.act(rt, banks[bi][:, :], AF.Relu, [R_bank[bi]], [rrt])
                    b.tt("pool", ft[:, pc * 4 + cb, :], rt, rt, ALU.mult, [rrt], [rft])
            for nb in range(4):
                wv, rw = stream["s"].get()
                for tile in range(4):
                    bi = next_obank()
                    for fc in range(16):
                        b.mm(banks[bi][:, :], ft[:, fc, tile * 128:(tile + 1) * 128], wv[:, fc, :], fc == 0, fc == 15,
                             [rft, rw], [R_bank[bi]])
                    dst = yacc[0][:, tile, nb * 512:(nb + 1) * 512]
                    if part == 0:
                        b.cp("dve", dst, banks[bi][:, :], [R_bank[bi]], [yaccR[tile]])
                    else:
                        b.tt("dve", dst, dst, banks[bi][:, :], ALU.add, [R_bank[bi], yaccR[tile]], [yaccR[tile]])
        for tile in range(4):
            b.act(junk[0], yacc[0][:, tile, :], AF.Square, [yaccR[tile]], [rs3[1]], accum_out=rs3[0][:, tile:tile + 1])
        rstd_from(rs3[0], rs3[1])
        for tile in range(4):
            xi, rxi = xin[tile % 2]
            r0 = t * SEG + tile * 128
            b.dma("sp", xi, out[r0:r0 + 128, :], [R_out[t][tile]], [rxi])
            b.stt("dve", yacc[0][:, tile, :], yacc[0][:, tile, :], rs3[0][:, tile:tile + 1], gmlp[0], ALU.mult, ALU.mult,
                  [yaccR[tile], rs3[1], gmlp[1]], [yaccR[tile]])
            b.tt("dve", xi, xi, yacc[0][:, tile, :], ALU.add, [yaccR[tile], rxi], [rxi])
            final_dmas.append(b.dma("pool", out[r0:r0 + 128, :], xi, [rxi], [R_out[t][tile]]))

    S.emit({"pool": final_dmas})
    print("ops:", {e: len(S.ops[e]) for e in ENGS})
    return nc


_CACHE = {}


def kernel(**inputs):
    x = np.asarray(inputs["x"], dtype=np.float32)
    L0 = lambda k: np.ascontiguousarray(np.asarray(inputs[k], dtype=np.float32)[0])
    w_in = L0("w_in")
    idx_u = np.concatenate([np.arange(g * 256, g * 256 + 128) for g in range(8)])
    idx_v = idx_u + 128
    perm = np.concatenate([idx_u, idx_v, np.arange(2048, 5120)])
    w_in_p = np.ascontiguousarray(w_in[:, perm])
    lam_in = np.stack([L0("lambda_q1"), L0("lambda_k1"), L0("lambda_q2"), L0("lambda_k2")]).astype(np.float32)
    common = {
        "w_in": w_in_p, "w_out": L0("w_out"), "w_up": L0("w_up"), "w_dn": L0("w_down"),
        "g_pre_mix": L0("pre_mix_g"), "g_post_mix": L0("post_mix_g"), "g_pre_mlp": L0("pre_mlp_g"),
        "g_post_mlp": L0("post_mlp_g"),
        "ln_g": L0("gmlp_ln_g").reshape(1024), "ln_b": L0("gmlp_ln_b").reshape(1024),
        "w_s": L0("gmlp_w_s"), "b_s": L0("gmlp_b_s").reshape(1024),
        "lam_in": lam_in, "subln": L0("diff_subln_g"),
    }
    in_maps = []
    for c in range(8):
        bb, j = c // 2, c % 2
        xb = x[bb].reshape(16, SEG, D)
        m = dict(common)
        m["x_own"] = np.ascontiguousarray(xb[OWN[j]].reshape(TOK, D))
        m["x_oth"] = np.ascontiguousarray(xb[OWN[1 - j]].reshape(TOK, D))
        m["dist"] = dist_table(j)
        in_maps.append(m)
    if "nc" not in _CACHE:
        _CACHE["nc"] = build_program()
    res = run_bass_kernel_spmd(_CACHE["nc"], in_maps, core_ids=list(range(8)))
    outp = np.empty((4, S_FULL, D), np.float32)
    for c in range(8):
        bb, j = c // 2, c % 2
        o = np.asarray(res.results[c]["out"], dtype=np.float32).reshape(8, SEG, D)
        ov = outp[bb].reshape(16, SEG, D)
        for t, s in enumerate(OWN[j]):
            ov[s] = o[t]
    _CACHE["last"] = res
    return outp
```

```python
import numpy as np
import ml_dtypes
import concourse.bass as bass
import concourse.mybir as mybir
from concourse.bass_utils import run_bass_kernel_spmd

F32 = mybir.dt.float32
BF = mybir.dt.bfloat16
AF = mybir.ActivationFunctionType
ALU = mybir.AluOpType
AX = mybir.AxisListType

D = 2048
SEG = 512
NSEG = 8
TOK = 4096
S_FULL = 8192
DFF = 8192
INW = 5120
EPS = 1e-6
OWN = {0: [0, 3, 4, 7, 8, 11, 12, 15], 1: [1, 2, 5, 6, 9, 10, 13, 14]}
SLOPES = [2.0 ** (-(i + 1)) for i in range(8)]
QK_SCALE = 0.125
LAMBDA_INIT = 0.2
NEG_BIG = -1.0e9
DEBUG = False
PHASE_LIMIT = "ABC"
N_SEG_A = 16

ENGS = ("pe", "act", "dve", "pool", "sp")


class Res:
    __slots__ = ("name", "last_w", "readers", "pw")

    def __init__(self, name):
        self.name = name
        self.last_w = None
        self.readers = {}
        self.pw = []


class Op:
    __slots__ = ("eng", "fn", "deps", "signal", "is_dma", "token", "idx", "slotprev")

    def __init__(self, eng, fn, is_dma):
        self.eng = eng
        self.fn = fn
        self.deps = []
        self.signal = False
        self.is_dma = is_dma
        self.token = None
        self.slotprev = None


class Sched:
    def __init__(self, nc, dma_slots=8, sem_limit=30000):
        self.nc = nc
        self.ops = {e: [] for e in ENGS}
        self.dma_slots = dma_slots
        self.sem_limit = sem_limit
        self.n_ops = 0

    def op(self, eng, fn, reads=(), writes=(), dma=False, pwrites=()):
        o = Op(eng, fn, dma)
        o.idx = self.n_ops
        self.n_ops += 1
        deps = []
        for r in reads:
            if r.last_w is not None:
                deps.append((r.last_w, 0))
            for pw in r.pw:
                deps.append((pw, 0))
        for w in writes:
            if w.last_w is not None:
                deps.append((w.last_w, 1))
            for pw in w.pw:
                deps.append((pw, 1))
            for rd in w.readers.values():
                deps.append((rd, 1))
        for w in pwrites:
            for rd in w.readers.values():
                deps.append((rd, 1))
        seen = set()
        for d, kind in deps:
            if d is o or id(d) in seen:
                continue
            if (not d.is_dma) and (not o.is_dma) and d.eng == o.eng:
                if o.eng == "pe" or kind != 0:
                    continue
            seen.add(id(d))
            o.deps.append(d)
            d.signal = True
        key = ("dma", o.idx) if dma else eng
        for r in reads:
            r.readers[key] = o
        for w in writes:
            w.last_w = o
            w.readers = {}
            w.pw = []
        for w in pwrites:
            w.pw.append(o)
        if dma:
            o.signal = True
        self.ops[eng].append(o)
        return o

    def emit(self, final_wait_ops):
        nc = self.nc
        sems = {}

        def getsem(name):
            if name not in sems:
                sems[name] = nc.alloc_semaphore(name)
            return sems[name]

        allops = sorted((o for e in ENGS for o in self.ops[e]), key=lambda o: o.idx)
        cnt = {e: 0 for e in ENGS}
        epoch = {e: 0 for e in ENGS}
        dma_n = {e: 0 for e in ENGS}
        slot_last = {}
        slot_cnt = {}
        slot_ep = {}
        for o in allops:
            if o.is_dma:
                s = dma_n[o.eng] % self.dma_slots
                dma_n[o.eng] += 1
                key = (o.eng, s)
                o.slotprev = slot_last.get(key)
                c = slot_cnt.get(key, 0) + 16
                if c > self.sem_limit:
                    slot_ep[key] = slot_ep.get(key, 0) + 1
                    c = 16
                slot_cnt[key] = c
                slot_last[key] = o
                o.token = ("d_%s_%d_%d" % (o.eng, s, slot_ep.get(key, 0)), c)
            elif o.signal:
                cnt[o.eng] += 1
                if cnt[o.eng] > self.sem_limit:
                    epoch[o.eng] += 1
                    cnt[o.eng] = 1
                o.token = ("c_%s_%d" % (o.eng, epoch[o.eng]), cnt[o.eng])
        for o in allops:
            if o.token is not None:
                getsem(o.token[0])
        with nc.Block() as block:
            def gen(ename):
                def body(eng):
                    waited = {}
                    for o in self.ops[ename]:
                        need = {}
                        deps = o.deps
                        if o.slotprev is not None:
                            deps = deps + [o.slotprev]
                        for d in deps:
                            sname, val = d.token
                            if waited.get(sname, 0) >= val:
                                continue
                            if need.get(sname, 0) < val:
                                need[sname] = val
                        for sname, val in need.items():
                            eng.wait_ge(sems[sname], val)
                            waited[sname] = val
                        ins = o.fn(eng)
                        if o.token is not None:
                            ins.then_inc(sems[o.token[0]], 16 if o.is_dma else 1)
                    for d in final_wait_ops.get(ename, ()):
                        sname, val = d.token
                        if waited.get(sname, 0) < val:
                            eng.wait_ge(sems[sname], val)
                            waited[sname] = val
                return body

            block.tensor(gen("pe"))
            block.scalar(gen("act"))
            block.vector(gen("dve"))
            block.gpsimd(gen("pool"))
            block.sync(gen("sp"))


class Arena:
    def __init__(self, nc, nbytes):
        self.t = nc.alloc_sbuf_tensor("arena", [128, nbytes // 2], BF)
        self.nbytes = nbytes
        self.regions = []
        self.off = 0

    def reset(self, off=0):
        self.off = off

    def alloc(self, name, free_shape, dtype):
        esz = 4 if dtype == F32 else 2
        n = 1
        for s in free_shape:
            n *= s
        nb = (n * esz + 63) // 64 * 64
        start = self.off
        end = start + nb
        assert end <= self.nbytes, "arena overflow %s %d > %d" % (name, end, self.nbytes)
        self.off = end
        v = self.t[:, start // 2: start // 2 + n * esz // 2]
        if dtype == F32:
            v = v.bitcast(F32)
        if len(free_shape) == 2:
            v = v.rearrange("p (a b) -> p a b", b=free_shape[1])
        elif len(free_shape) == 3:
            v = v.rearrange("p (a b c) -> p a b c", b=free_shape[1], c=free_shape[2])
        r = Res(name)
        for (s0, e0, r0) in self.regions:
            if s0 < end and start < e0:
                for k, o in r0.readers.items():
                    r.readers[("inh", id(o))] = o
                if r0.last_w is not None:
                    r.readers[("inh", id(r0.last_w))] = r0.last_w
        self.regions.append((start, end, r))
        return v, r

    def extra_res(self, base, names):
        out = []
        for (s0, e0, r0) in list(self.regions):
            if r0 is base:
                for nm in names:
                    r = Res(nm)
                    r.readers = dict(base.readers)
                    self.regions.append((s0, e0, r))
                    out.append(r)
        return out


class PieceStream:
    def __init__(self, b, ring, pieces):
        self.b = b
        self.ring = ring
        self.pieces = pieces
        self.i_load = 0
        self.i_get = 0

    def get(self):
        R = len(self.ring)
        n = self.i_get
        while self.i_load < len(self.pieces) and self.i_load <= n + R - 1:
            k = self.i_load
            wv, rw = self.ring[k % R]
            ap, res = self.pieces[k]
            self.b.dma("sp", wv, ap, [res], [rw])
            self.i_load += 1
        self.i_get += 1
        return self.ring[n % R]


def qtiles(h):
    w = 256 if h == 0 else 512
    return [(t, c, w) for t in range(NSEG) for c in range(0, SEG, w)]


def blocks_for(t, c, w):
    out = []
    for tp in range(t + 1):
        for kb in range(4):
            out.append((TOK + tp * SEG + kb * 128, 32 + tp * 4 + kb, 0, "oth", tp, kb))
    for tp in range(t):
        for kb in range(4):
            out.append((tp * SEG + kb * 128, tp * 4 + kb, 0, "own", tp, kb))
    for kb in range(4):
        if kb * 128 < c:
            out.append((t * SEG + kb * 128, t * 4 + kb, 0, "own", t, kb))
        elif kb * 128 < c + w:
            out.append((t * SEG + kb * 128, t * 4 + kb, kb * 128 - c, "diag", t, kb))
    return out


def dist_columns():
    cols = {}
    n = 0
    for w in (512, 256):
        for t in range(NSEG):
            for c in range(0, SEG, w):
                for i, _ in enumerate(blocks_for(t, c, w)):
                    cols[(w, t, c, i)] = n
                    n += 1
    return cols, n


def dist_table(j):
    cols, n = dist_columns()
    tab = np.zeros((128, n), np.float32)
    pos_own = [s * SEG for s in OWN[j]]
    pos_oth = [s * SEG for s in OWN[1 - j]]
    kl = np.arange(128, dtype=np.float32)
    for w in (512, 256):
        for t in range(NSEG):
            for c in range(0, SEG, w):
                qref = pos_own[t] + c + w // 2
                for i, (kcol, vblk, c0, kind, tp, kb) in enumerate(blocks_for(t, c, w)):
                    if kind == "oth":
                        if pos_oth[tp] > pos_own[t]:
                            tab[:, cols[(w, t, c, i)]] = NEG_BIG
                            continue
                        ks = pos_oth[tp] + kb * 128
                    else:
                        ks = pos_own[tp] + kb * 128
                    tab[:, cols[(w, t, c, i)]] = ks + kl - qref
    return tab


class B:
    def __init__(self, nc):
        self.nc = nc
        self.S = Sched(nc)
        self.pool_dmas = []

    def mm(self, out, lhsT, rhs, start, stop, reads, writes, skip=False):
        if skip:
            return self.S.op("pe", lambda e: e.matmul(out, lhsT=lhsT, rhs=rhs, start=start, stop=stop, skip_group_check=True),
                             reads, writes)
        return self.S.op("pe", lambda e: e.matmul(out, lhsT=lhsT, rhs=rhs, start=start, stop=stop), reads, writes)

    def tr(self, out, in_, ident, reads, writes):
        return self.S.op("pe", lambda e: e.transpose(out=out, in_=in_, identity=ident), reads, writes)

    def act(self, out, in_, func, reads, writes, bias=None, scale=None, accum_out=None):
        kw = {}
        if bias is not None:
            kw["bias"] = bias
        if scale is not None:
            kw["scale"] = scale
        if accum_out is not None:
            kw["accum_out"] = accum_out
        return self.S.op("act", lambda e: e.activation(out=out, in_=in_, func=func, **kw), reads, writes)

    def tsc(self, eng, out, in0, s1, s2, op0, op1, reads, writes):
        if op1 is None:
            return self.S.op(eng, lambda e: e.tensor_scalar(out=out, in0=in0, scalar1=s1, scalar2=None, op0=op0), reads, writes)
        return self.S.op(eng, lambda e: e.tensor_scalar(out=out, in0=in0, scalar1=s1, scalar2=s2, op0=op0, op1=op1), reads, writes)

    def tt(self, eng, out, in0, in1, op, reads, writes):
        return self.S.op(eng, lambda e: e.tensor_tensor(out=out, in0=in0, in1=in1, op=op), reads, writes)

    def stt(self, eng, out, in0, scalar, in1, op0, op1, reads, writes):
        return self.S.op(eng, lambda e: e.scalar_tensor_tensor(out=out, in0=in0, scalar=scalar, in1=in1, op0=op0, op1=op1), reads, writes)

    def cp(self, eng, out, in_, reads, writes):
        return self.S.op(eng, lambda e: e.tensor_copy(out=out, in_=in_), reads, writes)

    def red(self, eng, out, in_, reads, writes):
        return self.S.op(eng, lambda e: e.tensor_reduce(out=out, in_=in_, axis=AX.X, op=ALU.add), reads, writes)

    def recip(self, out, in_, reads, writes):
        return self.S.op("dve", lambda e: e.reciprocal(out=out, in_=in_), reads, writes)

    def memset(self, eng, ap, val, writes):
        return self.S.op(eng, lambda e: e.memset(ap, val), (), writes)

    def dma(self, q, out, in_, reads, writes, pwrites=(), slow=False):
        o = self._dma(q, out, in_, reads, writes, pwrites, slow)
        if q == "pool":
            self.pool_dmas.append(o)
        return o

    def _dma(self, q, out, in_, reads, writes, pwrites=(), slow=False):
        if slow:
            return self.S.op(q, lambda e: e.dma_start(out=out, in_=in_, allow_slow_non_contiguous=True), reads, writes,
                             dma=True, pwrites=pwrites)
        return self.S.op(q, lambda e: e.dma_start(out=out, in_=in_), reads, writes, dma=True, pwrites=pwrites)


def build_program():
    nc = bass.Bass("TRN2", target_bir_lowering=False)
    b = B(nc)
    S = b.S
    dk = "ExternalOutput" if DEBUG else "Internal"

    def din(name, shape):
        return nc.dram_tensor(name, list(shape), F32, kind="ExternalInput").ap()

    x_own = din("x_own", [TOK, D])
    x_oth = din("x_oth", [TOK, D])
    w_in = din("w_in", [D, INW])
    w_out = din("w_out", [D, D])
    w_up = din("w_up", [D, DFF])
    w_dn = din("w_dn", [DFF, D])
    g_pre_mix = din("g_pre_mix", [D])
    g_post_mix = din("g_post_mix", [D])
    g_pre_mlp = din("g_pre_mlp", [D])
    g_post_mlp = din("g_post_mlp", [D])
    ln_g = din("ln_g", [1024])
    ln_b = din("ln_b", [1024])
    w_s = din("w_s", [8, 128, 128])
    b_s = din("b_s", [1024])
    lam_in = din("lam_in", [4, 64])
    subln = din("subln", [128])
    cols, NDC = dist_columns()
    dist_in = din("dist", [128, NDC])
    out = nc.dram_tensor("out", [TOK, D], F32, kind="ExternalOutput").ap()

    win_bf = nc.dram_tensor("win_bf", [10, 128, 16, 512], BF).ap()
    wout_bf = nc.dram_tensor("wout_bf", [4, 128, 16, 512], BF).ap()
    wup_bf = nc.dram_tensor("wup_bf", [16, 128, 16, 512], BF).ap()
    wdn_bf = nc.dram_tensor("wdn_bf", [16, 128, 16, 512], BF).ap()
    q_scr = nc.dram_tensor("q_scr", [8, 128, TOK], BF, kind=dk).ap()
    k_scr = nc.dram_tensor("k_scr", [8, 128, S_FULL], BF, kind=dk).ap()
    v_scr = nc.dram_tensor("v_scr", [8, 128, 64, 128], BF, kind=dk).ap()
    y_scr = nc.dram_tensor("y_scr", [16, 128, TOK], BF, kind=dk).ap()

    R_win = [Res("win%d" % i) for i in range(10)]
    R_wout = [Res("wout%d" % i) for i in range(4)]
    R_wup = [Res("wup%d" % i) for i in range(16)]
    R_wdn = [Res("wdn%d" % i) for i in range(16)]
    R_q = [Res("qscr%d" % t) for t in range(8)]
    R_k = [Res("kscr%d" % t) for t in range(16)]
    R_v = [Res("vscr%d" % t) for t in range(16)]
    R_ya = [Res("ya%d" % t) for t in range(8)]
    R_yb = [[Res("yb%d_%d" % (h, t)) for t in range(8)] for h in range(8)]
    R_out = [[Res("out%d_%d" % (t, i)) for i in range(4)] for t in range(8)]

    psum_all = nc.alloc_psum_tensor("psum_all", [128, 8 * 512], F32)
    banks = [psum_all[:, i * 512:(i + 1) * 512] for i in range(8)]
    R_bank = [Res("bank%d" % i) for i in range(8)]

    def bank_bf(i):
        return banks[i].bitcast(BF)

    ident = nc.alloc_sbuf_tensor("ident", [128, 128], BF)
    ones_bf = nc.alloc_sbuf_tensor("ones_bf", [128, 128], BF)
    ones_f = nc.alloc_sbuf_tensor("ones_f", [128, 128], F32)
    tri = nc.alloc_sbuf_tensor("tri", [128, 128], BF)
    ctmp = nc.alloc_sbuf_tensor("ctmp", [128, 128], F32)
    gcol_in = nc.alloc_sbuf_tensor("gcol_in", [128, 16], F32)
    gcol_up = nc.alloc_sbuf_tensor("gcol_up", [128, 16], F32)
    gcol_sub = nc.alloc_sbuf_tensor("gcol_sub", [128, 1], F32)
    ones_col = nc.alloc_sbuf_tensor("ones_col", [128, 1], F32)
    eps_col = nc.alloc_sbuf_tensor("eps_col", [128, 1], F32)
    neg_lam = nc.alloc_sbuf_tensor("neg_lam", [128, 1], F32)
    lamt = nc.alloc_sbuf_tensor("lamt", [128, 4, 64], F32)
    lamp = nc.alloc_sbuf_tensor("lamp", [128, 2, 64], F32)
    lams = nc.alloc_sbuf_tensor("lams", [128, 2], F32)
    R_c = {n: Res(n) for n in "ident ones tri ctmp gcol_in gcol_up gcol_sub ones_col eps_col neg_lam lamt lamp lams".split()}

    b.memset("pool", ctmp[:], 1.0, [R_c["ctmp"]])
    S.op("pool", lambda e: e.affine_select(out=ctmp[:], in_=ctmp[:], pattern=[[1, 128]], compare_op=ALU.is_equal,
                                           fill=0.0, base=0, channel_multiplier=-1), [R_c["ctmp"]], [R_c["ctmp"]])
    b.cp("dve", ident[:], ctmp[:], [R_c["ctmp"]], [R_c["ident"]])
    b.memset("pool", ctmp[:], 1.0, [R_c["ctmp"]])
    S.op("pool", lambda e: e.affine_select(out=ctmp[:], in_=ctmp[:], pattern=[[1, 128]], compare_op=ALU.is_ge,
                                           fill=0.0, base=0, channel_multiplier=-1), [R_c["ctmp"]], [R_c["ctmp"]])
    b.cp("dve", tri[:], ctmp[:], [R_c["ctmp"]], [R_c["tri"]])
    b.memset("pool", ones_bf[:], 1.0, [R_c["ones"]])
    b.memset("pool", ones_f[:], 1.0, [R_c["ones"]])
    b.memset("pool", ones_col[:], 1.0, [R_c["ones_col"]])
    b.memset("pool", eps_col[:], EPS, [R_c["eps_col"]])
    b.dma("sp", gcol_in[:], g_pre_mix.rearrange("(c p) -> p c", p=128), [], [R_c["gcol_in"]], slow=True)
    b.dma("sp", gcol_up[:], g_pre_mlp.rearrange("(c p) -> p c", p=128), [], [R_c["gcol_up"]], slow=True)
    b.dma("sp", gcol_sub[:], subln.rearrange("(p o) -> p o", o=1), [], [R_c["gcol_sub"]], slow=True)
    b.tsc("dve", gcol_sub[:], gcol_sub[:], 1.0 - LAMBDA_INIT, None, ALU.mult, None, [R_c["gcol_sub"]], [R_c["gcol_sub"]])
    b.dma("sp", lamt[:], lam_in.partition_broadcast(128), [], [R_c["lamt"]])
    b.tt("dve", lamp[:, 0, :], lamt[:, 0, :], lamt[:, 1, :], ALU.mult, [R_c["lamt"]], [R_c["lamp"]])
    b.tt("dve", lamp[:, 1, :], lamt[:, 2, :], lamt[:, 3, :], ALU.mult, [R_c["lamt"]], [R_c["lamp"]])
    b.red("dve", lams[:], lamp[:], [R_c["lamp"]], [R_c["lams"]])
    b.act(lams[:], lams[:], AF.Exp, [R_c["lams"]], [R_c["lams"]])
    b.tsc("dve", neg_lam[:], lams[:, 1:2], -LAMBDA_INIT, None, ALU.add, None, [R_c["lams"]], [R_c["neg_lam"]])
    b.tt("dve", neg_lam[:], neg_lam[:], lams[:, 0:1], ALU.subtract, [R_c["lams"], R_c["neg_lam"]], [R_c["neg_lam"]])

    ar = Arena(nc, 200 * 1024)

    cast_units = []

    def add_units_rowmajor(src, dst, Rdst, ncols, colfn, rows_chunks, piece_of):
        for rc in range(rows_chunks):
            for u in range(ncols // 1024):
                s_ap = src[rc * 128:(rc + 1) * 128, u * 1024:(u + 1) * 1024]
                pieces = piece_of(rc, u)
                cast_units.append((s_ap, colfn(rc), [(dst[p, :, kc, :], Rdst[p]) for (p, kc) in pieces]))

    add_units_rowmajor(w_in, win_bf, R_win, INW, lambda rc: gcol_in[:, rc:rc + 1], 16,
                       lambda rc, u: [(2 * u, rc), (2 * u + 1, rc)])
    n_win_units = len(cast_units)
    add_units_rowmajor(w_out, wout_bf, R_wout, D, lambda rc: (None if rc < 8 else gcol_sub[:]), 16,
                       lambda rc, u: [(2 * u, rc), (2 * u + 1, rc)])
    add_units_rowmajor(w_up, wup_bf, R_wup, DFF, lambda rc: gcol_up[:, rc:rc + 1], 16,
                       lambda rc, u: [(2 * u, rc), (2 * u + 1, rc)])
    add_units_rowmajor(w_dn, wdn_bf, R_wdn, D, lambda rc: None, 64,
                       lambda rc, u: [((rc // 16) * 4 + 2 * u, rc % 16), ((rc // 16) * 4 + 2 * u + 1, rc % 16)])

    cast_state = {"i": 0}
    CS = 2
    cst_f = []
    cst_b = []

    def alloc_cast_staging():
        cst_f.clear()
        cst_b.clear()
        for i in range(CS):
            cst_f.append(ar.alloc("cst_f%d" % i, [1024], F32))
            cst_b.append(ar.alloc("cst_b%d" % i, [2, 512], BF))

    def do_casts(n, eng="dve"):
        for _ in range(n):
            i = cast_state["i"]
            if i >= len(cast_units):
                return
            cast_state["i"] = i + 1
            s_ap, col, dsts = cast_units[i]
            (f, rf) = cst_f[i % CS]
            (bt, rb) = cst_b[i % CS]
            b.dma("sp", f, s_ap, [], [rf])
            if col is None:
                b.cp("pool", bt.rearrange("p a b -> p (a b)"), f, [rf], [rb])
            elif eng == "act":
                b.act(bt.rearrange("p a b -> p (a b)"), f, AF.Copy, [rf, R_c["gcol_in"], R_c["gcol_up"], R_c["gcol_sub"], R_c["ones_col"]], [rb], scale=col)
            else:
                b.tsc(eng, bt.rearrange("p a b -> p (a b)"), f, col, None, ALU.mult, None,
                      [rf, R_c["gcol_in"], R_c["gcol_up"], R_c["gcol_sub"], R_c["ones_col"]], [rb])
            for k, (d_ap, d_res) in enumerate(dsts):
                b.dma("pool", d_ap, bt[:, k, :], [rb], [], pwrites=[d_res])

    ar.reset(0)
    alloc_cast_staging()
    cast_end = ar.off
    wmT = ar.alloc("wmT", [8, 128], BF)
    _off0 = ar.off
    wmf = ar.alloc("wmf", [8, 128], F32)
    wsb16 = ar.alloc("wsb16", [8, 128], BF)
    b.dma("sp", wmf[0], w_s.rearrange("g t s -> t g s"), [], [wmf[1]])
    b.cp("dve", wsb16[0], wmf[0], [wmf[1]], [wsb16[1]])
    for half in range(2):
        for g4 in range(4):
            g = half * 4 + g4
            b.tr(bank_bf(0)[:, g4 * 128:(g4 + 1) * 128], wsb16[0][:, g, :], ident[:], [wsb16[1], R_c["ident"]], [R_bank[0]])
        b.cp("dve", wmT[0][:, half * 4:half * 4 + 4, :], bank_bf(0)[:, 0:512].rearrange("p (g t) -> p g t", t=128),
             [R_bank[0]], [wmT[1]])
    for g in range(8):
        b.tt("pool", wmT[0][:, g, :], wmT[0][:, g, :], tri[:], ALU.mult, [wmT[1], R_c["tri"]], [wmT[1]])
    ar.reset(_off0)
    xin = [ar.alloc("xin%d" % i, [D], F32) for i in range(3)]
    junk = ar.alloc("junk", [D], BF)
    xs = [ar.alloc("xs%d" % i, [D], BF) for i in range(2)]
    xsT = []
    for s_ in range(2):
        v_, r_ = ar.alloc("xsT%d" % s_, [16, 512], BF)
        xsT.append((v_, ar.extra_res(r_, ["xsT%d_t%d" % (s_, t_) for t_ in range(4)])))
    wring = [ar.alloc("wring%d" % i, [16, 512], BF) for i in range(3)]
    uT = ar.alloc("uT", [8, 512], BF)
    vg = ar.alloc("vg", [4, 1024], F32)
    sqt = ar.alloc("sqt", [1024], F32)
    vln = [ar.alloc("vln%d" % i, [8, 128], BF) for i in range(4)]
    ctm = ar.alloc("ctm", [1024], F32)
    stg = [ar.alloc("stg%d" % i, [2048], BF) for i in range(4)]
    stg_ctr = {"i": 0}

    def next_stg():
        v_, r_ = stg[stg_ctr["i"] % 4]
        stg_ctr["i"] += 1
        return v_, r_
    bsb = ar.alloc("bsb", [1024], F32)
    lng = ar.alloc("lng", [1024], F32)
    lnb = ar.alloc("lnb", [1024], F32)
    ssA = [ar.alloc("ssA%d" % i, [4], F32) for i in range(2)]
    st1 = ar.alloc("st1", [32], F32)
    st2 = ar.alloc("st2", [32], F32)
    st3 = ar.alloc("st3", [32], F32)
    print("phase A arena bytes", ar.off)

    b.dma("sp", bsb[0], b_s.partition_broadcast(128), [], [bsb[1]])
    b.dma("sp", lng[0], ln_g.partition_broadcast(128), [], [lng[1]])
    b.dma("sp", lnb[0], ln_b.partition_broadcast(128), [], [lnb[1]])

    do_casts(n_win_units, eng="dve")
    CASTS_A = 6
    rest_per_iter = CASTS_A

    seglist = []
    for t in range(NSEG):
        seglist.append(("own", t))
        seglist.append(("oth", t))

    def seg_x(kind, t, tile):
        src = x_own if kind == "own" else x_oth
        r0 = t * SEG + tile * 128
        return src[r0:r0 + 128, :]

    xin_ctr = {"i": 0}

    xslots = {}

    def xL(U):
        i_, u_ = U // 8, U % 8
        if i_ >= N_SEG_A or U in xslots:
            return
        kind_, t_ = seglist[i_]
        xi, rxi = xin[xin_ctr["i"] % 3]
        xin_ctr["i"] += 1
        b.dma("act", xi, seg_x(kind_, t_, u_ % 4), [], [rxi])
        xslots[U] = (xi, rxi)

    def norm_items(i):
        kind, t = seglist[i]
        sl = i % 2
        ss, rss = ssA[sl]
        slots = xslots

        def useA(u_loc):
            u = 8 * i + u_loc
            xL(u)
            xL(u + 1)
            xL(u + 2)
            xi, rxi = slots[u]
            tile = u_loc % 4
            if u_loc < 4:
                b.act(junk[0], xi, AF.Square, [rxi], [junk[1], rss], accum_out=ss[:, tile:tile + 1])
                if u_loc == 3:
                    b.tsc("dve", ss, ss, 1.0 / D, EPS, ALU.mult, ALU.add, [rss], [rss])
                    b.act(ss, ss, AF.Sqrt, [rss], [rss])
                    b.recip(ss, ss, [rss], [rss])
                return
            xsb, rxs = xs[tile % 2]
            b.act(xsb, xi, AF.Copy, [rxi, rss], [rxs], scale=ss[:, tile:tile + 1])

        def useB(u_loc):
            tile = u_loc % 4
            xsb, rxs = xs[tile % 2]
            xtv, xtr = xsT[sl]
            for half in range(2):
                for c8 in range(8):
                    kc = half * 8 + c8
                    b.tr(bank_bf(half)[:, c8 * 128:(c8 + 1) * 128], xsb[:, kc * 128:(kc + 1) * 128], ident[:],
                         [rxs, R_c["ident"]], [R_bank[half]])
                dst = xtv[:, half * 8:half * 8 + 8, tile * 128:(tile + 1) * 128]
                src = bank_bf(half).rearrange("p (c t) -> p c t", t=128)
                b.act(dst, src, AF.Copy, [R_bank[half]], [xtr[tile]])

        items = []
        for u in range(4):
            items.append(lambda u=u: useA(u))
        items.append(lambda: useA(4))
        for u in range(4, 7):
            items.append(lambda u=u: (useB(u), useA(u + 1)))
        items.append(lambda: useB(7))
        return items

    pumpq = []

    def pump(n=1):
        for _ in range(n):
            if pumpq:
                pumpq.pop(0)()

    pieces_A = []
    for i_ in range(N_SEG_A):
        kind_, _t = seglist[i_]
        for pc_ in (range(10) if kind_ == "own" else range(6, 10)):
            pieces_A.append((win_bf[pc_], R_win[pc_]))
    stream = {"s": PieceStream(b, wring, pieces_A)}

    def load_piece_p(npump):
        pump(npump)
        return stream["s"].get()

    fb_ctr = {"i": 0}

    def next_fbank():
        i = 2 + (fb_ctr["i"] % 2)
        fb_ctr["i"] += 1
        return i

    tb_ctr = {"i": 0}

    def next_tbank():
        i = 4 + (tb_ctr["i"] % 4)
        tb_ctr["i"] += 1
        return i

    def fm_block(sl, wv, rw, cb, evac):
        bi = next_fbank()
        xtv, xtr = xsT[sl]
        for kc in range(16):
            b.mm(banks[bi][:, :], wv[:, kc, cb * 128:(cb + 1) * 128], xtv[:, kc, :],
                 kc == 0, kc == 15, [rw] + xtr, [R_bank[bi]])
        evac(bi)

    def tm_block(sl, wv, rw, tile, evac):
        bi = next_tbank()
        xtv, xtr = xsT[sl]
        for kc in range(16):
            b.mm(banks[bi][:, :], xtv[:, kc, tile * 128:(tile + 1) * 128], wv[:, kc, :], kc == 0, kc == 15,
                 [rw, xtr[tile]], [R_bank[bi]])
        evac(bi)

    def seg_compute(i):
        kind, t = seglist[i]
        sl = i % 2
        slot = t if kind == "own" else 8 + t
        NP = 2 if kind == "own" else 4
        if kind == "own":
            for pc in range(2):
                wv, rw = load_piece_p(NP)
                for cb in range(4):
                    g = pc * 4 + cb
                    fm_block(sl, wv, rw, cb, lambda bi, g=g: b.act(uT[0][:, g, :], banks[bi][:, :], AF.Gelu_apprx_tanh,
                                                                  [R_bank[bi]], [uT[1]]))
            for pc in range(2):
                wv, rw = load_piece_p(NP)
                for tile in range(4):
                    tm_block(sl, wv, rw, tile, lambda bi, tile=tile, pc=pc: b.act(
                        vg[0][:, tile, pc * 512:(pc + 1) * 512], banks[bi][:, :], AF.Gelu_apprx_tanh, [R_bank[bi]], [vg[1]]))
            vg3 = vg[0].rearrange("p a (g d) -> p (a g) d", d=128)
            b.red("dve", st1[0], vg3, [vg[1]], [st1[1]])
            for tile in range(4):
                b.tt("dve", sqt[0], vg[0][:, tile, :], vg[0][:, tile, :], ALU.mult, [vg[1]], [sqt[1]])
                b.red("dve", st2[0][:, tile * 8:(tile + 1) * 8], sqt[0].rearrange("p (g d) -> p g d", d=128), [sqt[1]], [st2[1]])
            b.tsc("dve", st1[0], st1[0], 1.0 / 128, None, ALU.mult, None, [st1[1]], [st1[1]])
            b.tt("dve", st3[0], st1[0], st1[0], ALU.mult, [st1[1]], [st3[1]])
            b.stt("dve", st2[0], st2[0], 1.0 / 128, st3[0], ALU.mult, ALU.subtract, [st2[1], st3[1]], [st2[1]])
            b.tsc("dve", st2[0], st2[0], EPS, None, ALU.add, None, [st2[1]], [st2[1]])
            b.act(st2[0], st2[0], AF.Sqrt, [st2[1]], [st2[1]])
            b.recip(st2[0], st2[0], [st2[1]], [st2[1]])
            for tile in range(4):
                vt = vg[0][:, tile, :].rearrange("p (g d) -> p g d", d=128)
                mb = st1[0][:, tile * 8:(tile + 1) * 8].unsqueeze(2).to_broadcast([128, 8, 128])
                rb_ = st2[0][:, tile * 8:(tile + 1) * 8].unsqueeze(2).to_broadcast([128, 8, 128])
                b.tt("dve", vt, vt, mb, ALU.subtract, [vg[1], st1[1]], [vg[1]])
                b.tt("dve", vt, vt, rb_, ALU.mult, [vg[1], st2[1]], [vg[1]])
                b.tt("dve", vg[0][:, tile, :], vg[0][:, tile, :], lng[0], ALU.mult, [vg[1], lng[1]], [vg[1]])
                vl, rvl = vln[tile]
                b.tt("dve", vl.rearrange("p g d -> p (g d)"), vg[0][:, tile, :], lnb[0], ALU.add, [vg[1], lnb[1]], [rvl])
            for pc in range(2):
                wv, rw = load_piece_p(NP)
                sv, sr = next_stg()
                sv3 = sv.rearrange("p (a b) -> p a b", b=512)
                for cb in range(4):
                    fm_block(sl, wv, rw, cb, lambda bi, cb=cb, sv3=sv3, sr=sr: b.act(sv3[:, cb, :], banks[bi][:, :], AF.Copy, [R_bank[bi]], [sr]))
                b.dma("pool", q_scr[pc * 4:(pc + 1) * 4, :, t * SEG:(t + 1) * SEG].rearrange("h p t -> p h t"), sv3, [sr], [],
                      pwrites=[R_q[t]])
        for pc in range(2):
            wv, rw = load_piece_p(NP)
            sv, sr = next_stg()
            sv3 = sv.rearrange("p (a b) -> p a b", b=512)
            for cb in range(4):
                fm_block(sl, wv, rw, cb, lambda bi, cb=cb, sv3=sv3, sr=sr: b.act(sv3[:, cb, :], banks[bi][:, :], AF.Copy, [R_bank[bi]], [sr]))
            b.dma("pool", k_scr[pc * 4:(pc + 1) * 4, :, slot * SEG:(slot + 1) * SEG].rearrange("h p t -> p h t"), sv3, [sr], [],
                  pwrites=[R_k[slot]])
        if kind == "own":
            for tile in range(4):
                vl, rvl = vln[tile]
                for half in range(2):
                    bi = next_fbank()
                    for g4 in range(4):
                        g = half * 4 + g4
                        b.mm(banks[bi][:, g4 * 128:(g4 + 1) * 128], vl[:, g, :], wmT[0][:, g, :], True, True,
                             [rvl, wmT[1]], [R_bank[bi]])
                    b.tt("dve", ctm[0][:, half * 512:(half + 1) * 512], banks[bi][:, :], bsb[0][:, half * 512:(half + 1) * 512],
                         ALU.add, [R_bank[bi], bsb[1]], [ctm[1]])
                if tile % 2 == 0:
                    ysv, ysr = next_stg()
                    ysv3 = ysv.rearrange("p (a b) -> p a b", b=256)
                b.tt("dve", ysv3[:, :, (tile % 2) * 128:(tile % 2 + 1) * 128], ctm[0].rearrange("p (g t) -> p g t", t=128),
                     uT[0][:, :, tile * 128:(tile + 1) * 128], ALU.mult, [ctm[1], uT[1]], [ysr])
                if tile % 2 == 1:
                    c_lo = t * SEG + (tile - 1) * 128
                    b.dma("pool", y_scr[0:8, :, c_lo:c_lo + 256].rearrange("c p t -> p c t"), ysv3, [ysr], [], pwrites=[R_ya[t]])
        for pc in range(2):
            wv, rw = load_piece_p(NP)
            sv, sr = next_stg()
            sv3 = sv.rearrange("p (a b) -> p a b", b=512)
            for tile in range(4):
                tm_block(sl, wv, rw, tile, lambda bi, tile=tile, sv3=sv3, sr=sr: b.act(
                    sv3[:, tile, :], banks[bi][:, :], AF.Copy, [R_bank[bi]], [sr]))
            for tile in range(4):
                b.dma("pool", v_scr[pc * 4:(pc + 1) * 4, :, slot * 4 + tile, :].rearrange("h p d -> p h d"),
                      sv3[:, tile, :].rearrange("p (h d) -> p h d", d=128), [sr], [], pwrites=[R_v[slot]])

    for it_ in norm_items(0):
        it_()
    for i in range(N_SEG_A):
        if i + 1 < N_SEG_A:
            pumpq.extend(norm_items(i + 1))
        ncast = rest_per_iter if PHASE_LIMIT != "A" else 0
        per = (ncast + 3) // 4
        for _ in range(4):
            pumpq.append(lambda per=per: do_casts(per, eng="dve"))
        seg_compute(i)
        pump(100)
    if PHASE_LIMIT == "A":
        S.emit({"pool": b.pool_dmas[-16:]})
        print("ops:", {e: len(S.ops[e]) for e in ENGS})
        return nc
    ar.reset(cast_end)
    kTb = [ar.alloc("kT%d" % i, [S_FULL], BF) for i in range(2)]
    Vb = [ar.alloc("V%d" % i, [64, 128], BF) for i in range(2)]
    qTb = [ar.alloc("qT%d" % i, [TOK], BF) for i in range(2)]
    Pb = [ar.alloc("P%d" % i, [2, 512], BF) for i in range(4)]
    e_r = [ar.alloc("e_r%d" % i, [512], F32) for i in range(2)]
    e_t = [ar.alloc("e_t%d" % i, [512], F32) for i in range(2)]
    e_o = ar.alloc("e_o", [512], F32)
    e_sq = ar.alloc("e_sq", [512], BF)
    e_rs = ar.alloc("e_rs", [512], F32)
    ybst = [ar.alloc("ybst%d" % i, [512], BF) for i in range(2)]
    Lacc = []
    for i_ in range(2):
        v_, r_ = ar.alloc("Lacc%d" % i_, [2, 512], F32)
        Lacc.append((v_, ar.extra_res(r_, ["Lacc%d_m%d" % (i_, m_) for m_ in range(2)])))
    e_ln = ar.alloc("e_ln", [512], F32)
    distt = ar.alloc("distt", [NDC], F32)
    biasT = ar.alloc("biasT", [8, 288], F32)
    bias0 = ar.alloc("bias0", [NDC - 288], F32)
    print("phase B arena bytes", ar.off)

    b.dma("sp", distt[0], dist_in, [], [distt[1]])
    for h in range(1, 8):
        b.tsc("dve", biasT[0][:, h, :], distt[0][:, 0:288], SLOPES[h], None, ALU.mult, None, [distt[1]], [biasT[1]])
    b.tsc("dve", bias0[0], distt[0][:, 288:NDC], SLOPES[0], None, ALU.mult, None, [distt[1]], [bias0[1]])

    SB = [(0, 1), (2, 3)]
    OBP = [(4, 5), (6, 7)]
    yb_ctr = {"i": 0}
    n_qt_total = sum(len(qtiles(h)) for h in range(8))
    casts_left = max(0, len(cast_units) - cast_state["i"])
    casts_per_qt = (casts_left + n_qt_total - 1) // n_qt_total

    def load_head(h):
        b.dma("sp", kTb[h % 2][0], k_scr[h], R_k, [kTb[h % 2][1]])
        b.dma("sp", Vb[h % 2][0], v_scr[h], R_v, [Vb[h % 2][1]])
        b.dma("sp", qTb[h % 2][0], q_scr[h], R_q, [qTb[h % 2][1]])

    tasks = []
    T_ = -1
    for h in range(8):
        for (t, c, w) in qtiles(h):
            T_ += 1
            blks = blocks_for(t, c, w)
            for i, blk in enumerate(blks):
                tasks.append((h, t, c, w, i, len(blks), blk, T_))

    def emit_qk(n):
        h, t, c, w, i, nb_, (kcol, vblk, c0, kind, tp, kb), T = tasks[n]
        kT, rkT = kTb[h % 2]
        qT, rqT = qTb[h % 2]
        sb = SB[n % 2]
        q0 = t * SEG + c
        nn = w - c0
        for m in range(2):
            b.mm(banks[sb[m]][:, 0:nn], kT[m * 64:(m + 1) * 64, kcol:kcol + 128],
                 qT[m * 64:(m + 1) * 64, q0 + c0:q0 + w], True, True, [rkT, rqT], [R_bank[sb[m]]])

    deferred = []

    OB = (4, 5)
    LB = (6, 7)
    e_l = [ar.alloc("e_l%d" % i, [512], F32) for i in range(2)]

    def epilogue0(n):
        h, t, c, w, i, nb_, blk, T = tasks[n]
        b.cp("dve", e_t[0][0][:, 0:w], banks[OB[0]][:, 0:w], [R_bank[OB[0]]], [e_t[0][1]])
        b.act(e_t[1][0][:, 0:w], banks[OB[1]][:, 0:w], AF.Copy, [R_bank[OB[1]]], [e_t[1][1]])
        b.cp("dve", e_l[0][0][:, 0:w], banks[LB[0]][:, 0:w], [R_bank[LB[0]]], [e_l[0][1]])
        b.act(e_l[1][0][:, 0:w], banks[LB[1]][:, 0:w], AF.Copy, [R_bank[LB[1]]], [e_l[1][1]])

    def epilogue1(n):
        h, t, c, w, i, nb_, blk, T = tasks[n]
        for m in (0, 1):
            b.recip(e_r[m][0][:, 0:w], e_l[m][0][:, 0:w], [e_l[m][1]], [e_r[m][1]])
        for m in (0, 1):
            b.tt("dve", e_t[m][0][:, 0:w], e_t[m][0][:, 0:w], e_r[m][0][:, 0:w], ALU.mult, [e_t[m][1], e_r[m][1]], [e_t[m][1]])
        b.stt("dve", e_o[0][:, 0:w], e_t[1][0][:, 0:w], neg_lam[:], e_t[0][0][:, 0:w], ALU.mult, ALU.add,
              [e_t[0][1], e_t[1][1], R_c["neg_lam"]], [e_o[1]])
        b.tt("dve", e_sq[0][:, 0:w], e_o[0][:, 0:w], e_o[0][:, 0:w], ALU.mult, [e_o[1]], [e_sq[1]])

    ssb = {"i": 0}

    def epilogue2(n):
        h, t, c, w, i, nb_, blk, T = tasks[n]
        q0 = t * SEG + c
        sbk = SB[(ssb["n"] + 1) % 2][0] if False else None
        bi = SB[ssb["cur"] % 2][0]
        b.mm(banks[bi][:, 0:w], ones_bf[:], e_sq[0][:, 0:w], True, True, [R_c["ones"], e_sq[1]], [R_bank[bi]])
        b.act(e_ln[0][:, 0:w], banks[bi][:, 0:w], AF.Ln, [R_bank[bi], R_c["eps_col"]], [e_ln[1]], bias=eps_col[:], scale=1.0 / 128)
        b.act(e_rs[0][:, 0:w], e_ln[0][:, 0:w], AF.Exp, [e_ln[1]], [e_rs[1]], scale=-0.5)
        yb, ryb = ybst[yb_ctr["i"] % 2]
        yb_ctr["i"] += 1
        b.tt("dve", yb[:, 0:w], e_o[0][:, 0:w], e_rs[0][:, 0:w], ALU.mult, [e_o[1], e_rs[1]], [ryb])
        b.dma("pool", y_scr[8 + h, :, q0:q0 + w], yb[:, 0:w], [ryb], [], pwrites=[R_yb[h][t]])
        do_casts(casts_per_qt, eng="dve")

    def emit_rest(n):
        h, t, c, w, i, nb_, (kcol, vblk, c0, kind, tp, kb), T = tasks[n]
        Vt, rV = Vb[h % 2]
        sb = SB[n % 2]
        Pt, rP = Pb[n % 4]
        nn = w - c0
        if h == 0:
            cc = cols[(256, t, c, i)] - 288
            bcol = bias0[0][:, cc:cc + 1]
            rbias = bias0[1]
        else:
            cc = cols[(512, t, c, i)]
            bcol = biasT[0][:, h, cc:cc + 1]
            rbias = biasT[1]
        for m in range(2):
            b.act(Pt[:, m, 0:nn], banks[sb[m]][:, 0:nn], AF.Exp, [R_bank[sb[m]], rbias], [rP], bias=bcol, scale=QK_SCALE)
        if kind == "diag":
            b.tt("dve", Pt[:, :, 0:128], Pt[:, :, 0:128], tri[:].unsqueeze(1).to_broadcast([128, 2, 128]), ALU.mult,
                 [rP, R_c["tri"]], [rP])
        for m in range(2):
            b.mm(banks[OB[m]][:, c0:w], Vt[:, vblk, :], Pt[:, m, 0:nn], i == 0, i == nb_ - 1, [rV, rP], [R_bank[OB[m]]])
            b.mm(banks[LB[m]][:, c0:w], ones_bf[:], Pt[:, m, 0:nn], i == 0, i == nb_ - 1, [R_c["ones"], rP], [R_bank[LB[m]]])
        if i == nb_ - 1:
            epilogue0(n)
            deferred.append((n + 3, lambda n=n: epilogue1(n)))
            deferred.append((n + 6, lambda n=n: epilogue2(n)))
        ssb["cur"] = n
        deferred.sort(key=lambda x: x[0])
        while deferred and deferred[0][0] <= n:
            deferred.pop(0)[1]()

    load_head(0)
    NT_ = len(tasks)
    emit_qk(0)
    for n in range(NT_):
        h = tasks[n][0]
        if tasks[n][4] == 0 and tasks[n][1] == 0 and tasks[n][2] == 0 and h + 1 < 8:
            load_head(h + 1)
        if n + 1 < NT_:
            emit_qk(n + 1)
        emit_rest(n)
    while deferred:
        deferred.pop(0)[1]()
    do_casts(10 ** 6, eng="dve")

    if PHASE_LIMIT == "AB":
        S.emit({"pool": b.pool_dmas[-16:]})
        print("ops:", {e: len(S.ops[e]) for e in ENGS})
        return nc
    ar.reset(0)
    wring = [ar.alloc("wringC%d" % i, [16, 512], BF) for i in range(3)]
    pieces_C = []
    for t_ in range(NSEG):
        for nb_ in range(4):
            pieces_C.append((wout_bf[nb_], R_wout[nb_]))
        for part_ in range(4):
            for pc_ in range(4):
                pieces_C.append((wup_bf[part_ * 4 + pc_], R_wup[part_ * 4 + pc_]))
            for nb_ in range(4):
                pieces_C.append((wdn_bf[part_ * 4 + nb_], R_wdn[part_ * 4 + nb_]))
    stream["s"] = PieceStream(b, wring, pieces_C)
    yT = ar.alloc("yT", [16, 512], BF)
    _yv, _yr = ar.alloc("yacc", [4, D], F32)
    yacc = (_yv, None)
    yaccR = ar.extra_res(_yr, ["yacc_t%d" % i for i in range(4)])
    xin = [ar.alloc("xinC%d" % i, [D], F32) for i in range(2)]
    junk = ar.alloc("junkC", [D], BF)
    hn = [ar.alloc("hn%d" % i, [D], BF) for i in range(2)]
    hnT = ar.alloc("hnT", [16, 512], BF)
    f1T = [ar.alloc("f1T%d" % i, [16, 512], BF) for i in range(2)]
    rtmp = [ar.alloc("rtmp%d" % i, [512], F32) for i in range(2)]
    gpost = ar.alloc("gpost", [D], F32)
    gmlp = ar.alloc("gmlp", [D], F32)
    ssq = ar.alloc("ssq", [16], F32)
    rs1 = ar.alloc("rs1", [4], F32)
    rs2 = [ar.alloc("rs2_%d" % i, [1], F32) for i in range(4)]
    rs3 = ar.alloc("rs3", [4], F32)
    print("phase C arena bytes", ar.off)
    b.dma("sp", gpost[0], g_post_mix.partition_broadcast(128), [], [gpost[1]])
    b.dma("sp", gmlp[0], g_post_mlp.partition_broadcast(128), [], [gmlp[1]])

    ub_ctr = {"i": 0}
    ob_ctr = {"i": 0}
    final_dmas = []

    def next_obank():
        i = 4 + ob_ctr["i"] % 3
        ob_ctr["i"] += 1
        return i

    def rstd_from(ssv, rss_):
        b.tsc("dve", ssv, ssv, 1.0 / D, EPS, ALU.mult, ALU.add, [rss_], [rss_])
        b.act(ssv, ssv, AF.Sqrt, [rss_], [rss_])
        b.recip(ssv, ssv, [rss_], [rss_])

    def load_yT(t):
        yres = [R_ya[t]] + [R_yb[h][t] for h in range(8)]
        b.dma("sp", yT[0], y_scr[:, :, t * SEG:(t + 1) * SEG].rearrange("c p t -> p c t"), yres, [yT[1]])

    load_yT(0)
    for t in range(NSEG):
        for nb in range(4):
            wv, rw = stream["s"].get()
            for tile in range(4):
                bi = next_obank()
                for kc in range(16):
                    b.mm(banks[bi][:, :], yT[0][:, kc, tile * 128:(tile + 1) * 128], wv[:, kc, :], kc == 0, kc == 15,
                         [yT[1], rw], [R_bank[bi]])
                b.cp("dve", yacc[0][:, tile, nb * 512:(nb + 1) * 512], banks[bi][:, :], [R_bank[bi]], [yaccR[tile]])
                b.act(junk[0][:, 0:512], yacc[0][:, tile, nb * 512:(nb + 1) * 512], AF.Square, [yaccR[tile]], [ssq[1]],
                      accum_out=ssq[0][:, tile * 4 + nb:tile * 4 + nb + 1])
        if t + 1 < NSEG:
            load_yT(t + 1)
        b.red("dve", rs1[0], ssq[0].rearrange("p (a b) -> p a b", b=4), [ssq[1]], [rs1[1]])
        rstd_from(rs1[0], rs1[1])
        xsl = {}

        def xl(tile, t=t):
            xi, rxi = xin[tile % 2]
            r0 = t * SEG + tile * 128
            b.dma("sp", xi, x_own[r0:r0 + 128, :], [], [rxi])
            xsl[tile] = (xi, rxi)

        xl(0)
        for tile in range(4):
            if tile + 1 < 4:
                xl(tile + 1)
            xi, rxi = xsl[tile]
            r0 = t * SEG + tile * 128
            b.stt("dve", yacc[0][:, tile, :], yacc[0][:, tile, :], rs1[0][:, tile:tile + 1], gpost[0], ALU.mult, ALU.mult,
                  [yaccR[tile], rs1[1], gpost[1]], [yaccR[tile]])
            b.tt("pool", yacc[0][:, tile, :], yacc[0][:, tile, :], xi, ALU.add, [yaccR[tile], rxi], [yaccR[tile]])
            b.dma("pool", out[r0:r0 + 128, :], yacc[0][:, tile, :], [yaccR[tile]], [R_out[t][tile]])
            r2c, rr2 = rs2[tile]
            b.act(junk[0], yacc[0][:, tile, :], AF.Square, [yaccR[tile]], [rr2], accum_out=r2c)
        for tile in range(4):
            r2c, rr2 = rs2[tile]
            b.tsc("dve", r2c, r2c, 1.0 / D, EPS, ALU.mult, ALU.add, [rr2], [rr2])
            b.act(r2c, r2c, AF.Sqrt, [rr2], [rr2])
            b.recip(r2c, r2c, [rr2], [rr2])
        for tile in range(4):
            r2c, rr2 = rs2[tile]
            hb, rhb = hn[tile % 2]
            b.act(hb, yacc[0][:, tile, :], AF.Copy, [yaccR[tile], rr2], [rhb], scale=r2c)
            for half in range(2):
                for c8 in range(8):
                    kc = half * 8 + c8
                    b.tr(bank_bf(half)[:, c8 * 128:(c8 + 1) * 128], hb[:, kc * 128:(kc + 1) * 128], ident[:],
                         [rhb, R_c["ident"]], [R_bank[half]])
                b.cp("dve", hnT[0][:, half * 8:half * 8 + 8, tile * 128:(tile + 1) * 128],
                     bank_bf(half).rearrange("p (c t) -> p c t", t=128), [R_bank[half]], [hnT[1]])
        for part in range(4):
            ft, rft = f1T[part % 2]
            for pc in range(4):
                wv, rw = stream["s"].get()
                for cb in range(4):
                    bi = 2 + ub_ctr["i"] % 2
                    ub_ctr["i"] += 1
                    for kc in range(16):
                        b.mm(banks[bi][:, :], wv[:, kc, cb * 128:(cb + 1) * 128], hnT[0][:, kc, :], kc == 0, kc == 15,
                             [rw, hnT[1]], [R_bank[bi]])
                    rt, rrt = rtmp[ub_ctr["i"] % 2]
                    b.act(rt, banks[bi][:, :], AF.Relu, [R_bank[bi]], [rrt])
                    b.tt("pool", ft[:, pc * 4 + cb, :], rt, rt, ALU.mult, [rrt], [rft])
            for nb in range(4):
                wv, rw = stream["s"].get()
                for tile in range(4):
                    bi = next_obank()
                    for fc in range(16):
                        b.mm(banks[bi][:, :], ft[:, fc, tile * 128:(tile + 1) * 128], wv[:, fc, :], fc == 0, fc == 15,
                             [rft, rw], [R_bank[bi]])
                    dst = yacc[0][:, tile, nb * 512:(nb + 1) * 512]
                    if part == 0:
                        b.cp("dve", dst, banks[bi][:, :], [R_bank[bi]], [yaccR[tile]])
                    else:
                        b.tt("dve", dst, dst, banks[bi][:, :], ALU.add, [R_bank[bi], yaccR[tile]], [yaccR[tile]])
        for tile in range(4):
            b.act(junk[0], yacc[0][:, tile, :], AF.Square, [yaccR[tile]], [rs3[1]], accum_out=rs3[0][:, tile:tile + 1])
        rstd_from(rs3[0], rs3[1])
        for tile in range(4):
            xi, rxi = xin[tile % 2]
            r0 = t * SEG + tile * 128
            b.dma("sp", xi, out[r0:r0 + 128, :], [R_out[t][tile]], [rxi])
            b.stt("dve", yacc[0][:, tile, :], yacc[0][:, tile, :], rs3[0][:, tile:tile + 1], gmlp[0], ALU.mult, ALU.mult,
                  [yaccR[tile], rs3[1], gmlp[1]], [yaccR[tile]])
            b.tt("dve", xi, xi, yacc[0][:, tile, :], ALU.add, [yaccR[tile], rxi], [rxi])
            final_dmas.append(b.dma("pool", out[r0:r0 + 128, :], xi, [rxi], [R_out[t][tile]]))

    S.emit({"pool": final_dmas})
    print("ops:", {e: len(S.ops[e]) for e in ENGS})
    return nc


_CACHE = {}


def kernel(**inputs):
    x = np.asarray(inputs["x"], dtype=np.float32)
    L0 = lambda k: np.ascontiguousarray(np.asarray(inputs[k], dtype=np.float32)[0])
    w_in = L0("w_in")
    idx_u = np.concatenate([np.arange(g * 256, g * 256 + 128) for g in range(8)])
    idx_v = idx_u + 128
    perm = np.concatenate([idx_u, idx_v, np.arange(2048, 5120)])
    w_in_p = np.ascontiguousarray(w_in[:, perm])
    lam_in = np.stack([L0("lambda_q1"), L0("lambda_k1"), L0("lambda_q2"), L0("lambda_k2")]).astype(np.float32)
    common = {
        "w_in": w_in_p, "w_out": L0("w_out"), "w_up": L0("w_up"), "w_dn": L0("w_down"),
        "g_pre_mix": L0("pre_mix_g"), "g_post_mix": L0("post_mix_g"), "g_pre_mlp": L0("pre_mlp_g"),
        "g_post_mlp": L0("post_mlp_g"),
        "ln_g": L0("gmlp_ln_g").reshape(1024), "ln_b": L0("gmlp_ln_b").reshape(1024),
        "w_s": L0("gmlp_w_s"), "b_s": L0("gmlp_b_s").reshape(1024),
        "lam_in": lam_in, "subln": L0("diff_subln_g"),
    }
    in_maps = []
    for c in range(8):
        bb, j = c // 2, c % 2
        xb = x[bb].reshape(16, SEG, D)
        m = dict(common)
        m["x_own"] = np.ascontiguousarray(xb[OWN[j]].reshape(TOK, D))
        m["x_oth"] = np.ascontiguousarray(xb[OWN[1 - j]].reshape(TOK, D))
        m["dist"] = dist_table(j)
        in_maps.append(m)
    if "nc" not in _CACHE:
        _CACHE["nc"] = build_program()
    res = run_bass_kernel_spmd(_CACHE["nc"], in_maps, core_ids=list(range(8)))
    outp = np.empty((4, S_FULL, D), np.float32)
    for c in range(8):
        bb, j = c // 2, c % 2
        o = np.asarray(res.results[c]["out"], dtype=np.float32).reshape(8, SEG, D)
        ov = outp[bb].reshape(16, SEG, D)
        for t, s in enumerate(OWN[j]):
            ov[s] = o[t]
    _CACHE["last"] = res
    return outp
```

```python
import numpy as np
import ml_dtypes
import concourse.bass as bass
import concourse.mybir as mybir
from concourse.bass_utils import run_bass_kernel_spmd

F32 = mybir.dt.float32
BF = mybir.dt.bfloat16
AF = mybir.ActivationFunctionType
ALU = mybir.AluOpType
AX = mybir.AxisListType

D = 2048
SEG = 512
NSEG = 8
TOK = 4096
S_FULL = 8192
DFF = 8192
INW = 5120
EPS = 1e-6
OWN = {0: [0, 3, 4, 7, 8, 11, 12, 15], 1: [1, 2, 5, 6, 9, 10, 13, 14]}
SLOPES = [2.0 ** (-(i + 1)) for i in range(8)]
QK_SCALE = 0.125
LAMBDA_INIT = 0.2
NEG_BIG = -1.0e9
DEBUG = False
PHASE_LIMIT = "ABC"
N_SEG_A = 16

ENGS = ("pe", "act", "dve", "pool", "sp")


class Res:
    __slots__ = ("name", "last_w", "readers", "pw")

    def __init__(self, name):
        self.name = name
        self.last_w = None
        self.readers = {}
        self.pw = []


class Op:
    __slots__ = ("eng", "fn", "deps", "signal", "is_dma", "token", "idx", "slotprev")

    def __init__(self, eng, fn, is_dma):
        self.eng = eng
        self.fn = fn
        self.deps = []
        self.signal = False
        self.is_dma = is_dma
        self.token = None
        self.slotprev = None


class Sched:
    def __init__(self, nc, dma_slots=8, sem_limit=30000):
        self.nc = nc
        self.ops = {e: [] for e in ENGS}
        self.dma_slots = dma_slots
        self.sem_limit = sem_limit
        self.n_ops = 0

    def op(self, eng, fn, reads=(), writes=(), dma=False, pwrites=()):
        o = Op(eng, fn, dma)
        o.idx = self.n_ops
        self.n_ops += 1
        deps = []
        for r in reads:
            if r.last_w is not None:
                deps.append((r.last_w, 0))
            for pw in r.pw:
                deps.append((pw, 0))
        for w in writes:
            if w.last_w is not None:
                deps.append((w.last_w, 1))
            for pw in w.pw:
                deps.append((pw, 1))
            for rd in w.readers.values():
                deps.append((rd, 1))
        for w in pwrites:
            for rd in w.readers.values():
                deps.append((rd, 1))
        seen = set()
        for d, kind in deps:
            if d is o or id(d) in seen:
                continue
            if (not d.is_dma) and (not o.is_dma) and d.eng == o.eng:
                if o.eng == "pe" or kind != 0:
                    continue
            seen.add(id(d))
            o.deps.append(d)
            d.signal = True
        key = ("dma", o.idx) if dma else eng
        for r in reads:
            r.readers[key] = o
        for w in writes:
            w.last_w = o
            w.readers = {}
            w.pw = []
        for w in pwrites:
            w.pw.append(o)
        if dma:
            o.signal = True
        self.ops[eng].append(o)
        return o

    def emit(self, final_wait_ops):
        nc = self.nc
        sems = {}

        def getsem(name):
            if name not in sems:
                sems[name] = nc.alloc_semaphore(name)
            return sems[name]

        allops = sorted((o for e in ENGS for o in self.ops[e]), key=lambda o: o.idx)
        cnt = {e: 0 for e in ENGS}
        epoch = {e: 0 for e in ENGS}
        dma_n = {e: 0 for e in ENGS}
        slot_last = {}
        slot_cnt = {}
        slot_ep = {}
        for o in allops:
            if o.is_dma:
                s = dma_n[o.eng] % self.dma_slots
                dma_n[o.eng] += 1
                key = (o.eng, s)
                o.slotprev = slot_last.get(key)
                c = slot_cnt.get(key, 0) + 16
                if c > self.sem_limit:
                    slot_ep[key] = slot_ep.get(key, 0) + 1
                    c = 16
                slot_cnt[key] = c
                slot_last[key] = o
                o.token = ("d_%s_%d_%d" % (o.eng, s, slot_ep.get(key, 0)), c)
            elif o.signal:
                cnt[o.eng] += 1
                if cnt[o.eng] > self.sem_limit:
                    epoch[o.eng] += 1
                    cnt[o.eng] = 1
                o.token = ("c_%s_%d" % (o.eng, epoch[o.eng]), cnt[o.eng])
        for o in allops:
            if o.token is not None:
                getsem(o.token[0])
        with nc.Block() as block:
            def gen(ename):
                def body(eng):
                    waited = {}
                    for o in self.ops[ename]:
                        need = {}
                        deps = o.deps
                        if o.slotprev is not None:
                            deps = deps + [o.slotprev]
                        for d in deps:
                            sname, val = d.token
                            if waited.get(sname, 0) >= val:
                                continue
                            if need.get(sname, 0) < val:
                                need[sname] = val
                        for sname, val in need.items():
                            eng.wait_ge(sems[sname], val)
                            waited[sname] = val
                        ins = o.fn(eng)
                        if o.token is not None:
                            ins.then_inc(sems[o.token[0]], 16 if o.is_dma else 1)
                    for d in final_wait_ops.get(ename, ()):
                        sname, val = d.token
                        if waited.get(sname, 0) < val:
                            eng.wait_ge(sems[sname], val)
                            waited[sname] = val
                return body

            block.tensor(gen("pe"))
            block.scalar(gen("act"))
            block.vector(gen("dve"))
            block.gpsimd(gen("pool"))
            block.sync(gen("sp"))


class Arena:
    def __init__(self, nc, nbytes):
        self.t = nc.alloc_sbuf_tensor("arena", [128, nbytes // 2], BF)
        self.nbytes = nbytes
        self.regions = []
        self.off = 0

    def reset(self, off=0):
        self.off = off

    def alloc(self, name, free_shape, dtype):
        esz = 4 if dtype == F32 else 2
        n = 1
        for s in free_shape:
            n *= s
        nb = (n * esz + 63) // 64 * 64
        start = self.off
        end = start + nb
        assert end <= self.nbytes, "arena overflow %s %d > %d" % (name, end, self.nbytes)
        self.off = end
        v = self.t[:, start // 2: start // 2 + n * esz // 2]
        if dtype == F32:
            v = v.bitcast(F32)
        if len(free_shape) == 2:
            v = v.rearrange("p (a b) -> p a b", b=free_shape[1])
        elif len(free_shape) == 3:
            v = v.rearrange("p (a b c) -> p a b c", b=free_shape[1], c=free_shape[2])
        r = Res(name)
        for (s0, e0, r0) in self.regions:
            if s0 < end and start < e0:
                for k, o in r0.readers.items():
                    r.readers[("inh", id(o))] = o
                if r0.last_w is not None:
                    r.readers[("inh", id(r0.last_w))] = r0.last_w
        self.regions.append((start, end, r))
        return v, r

    def extra_res(self, base, names):
        out = []
        for (s0, e0, r0) in list(self.regions):
            if r0 is base:
                for nm in names:
                    r = Res(nm)
                    r.readers = dict(base.readers)
                    self.regions.append((s0, e0, r))
                    out.append(r)
        return out


class PieceStream:
    def __init__(self, b, ring, pieces):
        self.b = b
        self.ring = ring
        self.pieces = pieces
        self.i_load = 0
        self.i_get = 0

    def get(self):
        R = len(self.ring)
        n = self.i_get
        while self.i_load < len(self.pieces) and self.i_load <= n + R - 1:
            k = self.i_load
            wv, rw = self.ring[k % R]
            ap, res = self.pieces[k]
            self.b.dma("sp", wv, ap, [res], [rw])
            self.i_load += 1
        self.i_get += 1
        return self.ring[n % R]


def qtiles(h):
    w = 256 if h == 0 else 512
    return [(t, c, w) for t in range(NSEG) for c in range(0, SEG, w)]


def blocks_for(t, c, w):
    out = []
    for tp in range(t + 1):
        for kb in range(4):
            out.append((TOK + tp * SEG + kb * 128, 32 + tp * 4 + kb, 0, "oth", tp, kb))
    for tp in range(t):
        for kb in range(4):
            out.append((tp * SEG + kb * 128, tp * 4 + kb, 0, "own", tp, kb))
    for kb in range(4):
        if kb * 128 < c:
            out.append((t * SEG + kb * 128, t * 4 + kb, 0, "own", t, kb))
        elif kb * 128 < c + w:
            out.append((t * SEG + kb * 128, t * 4 + kb, kb * 128 - c, "diag", t, kb))
    return out


def dist_columns():
    cols = {}
    n = 0
    for w in (512, 256):
        for t in range(NSEG):
            for c in range(0, SEG, w):
                for i, _ in enumerate(blocks_for(t, c, w)):
                    cols[(w, t, c, i)] = n
                    n += 1
    return cols, n


def dist_table(j):
    cols, n = dist_columns()
    tab = np.zeros((128, n), np.float32)
    pos_own = [s * SEG for s in OWN[j]]
    pos_oth = [s * SEG for s in OWN[1 - j]]
    kl = np.arange(128, dtype=np.float32)
    for w in (512, 256):
        for t in range(NSEG):
            for c in range(0, SEG, w):
                qref = pos_own[t] + c + w // 2
                for i, (kcol, vblk, c0, kind, tp, kb) in enumerate(blocks_for(t, c, w)):
                    if kind == "oth":
                        if pos_oth[tp] > pos_own[t]:
                            tab[:, cols[(w, t, c, i)]] = NEG_BIG
                            continue
                        ks = pos_oth[tp] + kb * 128
                    else:
                        ks = pos_own[tp] + kb * 128
                    tab[:, cols[(w, t, c, i)]] = ks + kl - qref
    return tab


class B:
    def __init__(self, nc):
        self.nc = nc
        self.S = Sched(nc)
        self.pool_dmas = []

    def mm(self, out, lhsT, rhs, start, stop, reads, writes, skip=False):
        if skip:
            return self.S.op("pe", lambda e: e.matmul(out, lhsT=lhsT, rhs=rhs, start=start, stop=stop, skip_group_check=True),
                             reads, writes)
        return self.S.op("pe", lambda e: e.matmul(out, lhsT=lhsT, rhs=rhs, start=start, stop=stop), reads, writes)

    def tr(self, out, in_, ident, reads, writes):
        return self.S.op("pe", lambda e: e.transpose(out=out, in_=in_, identity=ident), reads, writes)

    def act(self, out, in_, func, reads, writes, bias=None, scale=None, accum_out=None):
        kw = {}
        if bias is not None:
            kw["bias"] = bias
        if scale is not None:
            kw["scale"] = scale
        if accum_out is not None:
            kw["accum_out"] = accum_out
        return self.S.op("act", lambda e: e.activation(out=out, in_=in_, func=func, **kw), reads, writes)

    def tsc(self, eng, out, in0, s1, s2, op0, op1, reads, writes):
        if op1 is None:
            return self.S.op(eng, lambda e: e.tensor_scalar(out=out, in0=in0, scalar1=s1, scalar2=None, op0=op0), reads, writes)
        return self.S.op(eng, lambda e: e.tensor_scalar(out=out, in0=in0, scalar1=s1, scalar2=s2, op0=op0, op1=op1), reads, writes)

    def tt(self, eng, out, in0, in1, op, reads, writes):
        return self.S.op(eng, lambda e: e.tensor_tensor(out=out, in0=in0, in1=in1, op=op), reads, writes)

    def stt(self, eng, out, in0, scalar, in1, op0, op1, reads, writes):
        return self.S.op(eng, lambda e: e.scalar_tensor_tensor(out=out, in0=in0, scalar=scalar, in1=in1, op0=op0, op1=op1), reads, writes)

    def cp(self, eng, out, in_, reads, writes):
        return self.S.op(eng, lambda e: e.tensor_copy(out=out, in_=in_), reads, writes)

    def red(self, eng, out, in_, reads, writes):
        return self.S.op(eng, lambda e: e.tensor_reduce(out=out, in_=in_, axis=AX.X, op=ALU.add), reads, writes)

    def recip(self, out, in_, reads, writes):
        return self.S.op("dve", lambda e: e.reciprocal(out=out, in_=in_), reads, writes)

    def memset(self, eng, ap, val, writes):
        return self.S.op(eng, lambda e: e.memset(ap, val), (), writes)

    def dma(self, q, out, in_, reads, writes, pwrites=(), slow=False):
        o = self._dma(q, out, in_, reads, writes, pwrites, slow)
        if q == "pool":
            self.pool_dmas.append(o)
        return o

    def _dma(self, q, out, in_, reads, writes, pwrites=(), slow=False):
        if slow:
            return self.S.op(q, lambda e: e.dma_start(out=out, in_=in_, allow_slow_non_contiguous=True), reads, writes,
                             dma=True, pwrites=pwrites)
        return self.S.op(q, lambda e: e.dma_start(out=out, in_=in_), reads, writes, dma=True, pwrites=pwrites)


def build_program():
    nc = bass.Bass("TRN2", target_bir_lowering=False)
    b = B(nc)
    S = b.S
    dk = "ExternalOutput" if DEBUG else "Internal"

    def din(name, shape):
        return nc.dram_tensor(name, list(shape), F32, kind="ExternalInput").ap()

    x_own = din("x_own", [TOK, D])
    x_oth = din("x_oth", [TOK, D])
    w_in = din("w_in", [D, INW])
    w_out = din("w_out", [D, D])
    w_up = din("w_up", [D, DFF])
    w_dn = din("w_dn", [DFF, D])
    g_pre_mix = din("g_pre_mix", [D])
    g_post_mix = din("g_post_mix", [D])
    g_pre_mlp = din("g_pre_mlp", [D])
    g_post_mlp = din("g_post_mlp", [D])
    ln_g = din("ln_g", [1024])
    ln_b = din("ln_b", [1024])
    w_s = din("w_s", [8, 128, 128])
    b_s = din("b_s", [1024])
    lam_in = din("lam_in", [4, 64])
    subln = din("subln", [128])
    cols, NDC = dist_columns()
    dist_in = din("dist", [128, NDC])
    out = nc.dram_tensor("out", [TOK, D], F32, kind="ExternalOutput").ap()

    win_bf = nc.dram_tensor("win_bf", [10, 128, 16, 512], BF).ap()
    wout_bf = nc.dram_tensor("wout_bf", [4, 128, 16, 512], BF).ap()
    wup_bf = nc.dram_tensor("wup_bf", [16, 128, 16, 512], BF).ap()
    wdn_bf = nc.dram_tensor("wdn_bf", [16, 128, 16, 512], BF).ap()
    q_scr = nc.dram_tensor("q_scr", [8, 128, TOK], BF, kind=dk).ap()
    k_scr = nc.dram_tensor("k_scr", [8, 128, S_FULL], BF, kind=dk).ap()
    v_scr = nc.dram_tensor("v_scr", [8, 128, 64, 128], BF, kind=dk).ap()
    y_scr = nc.dram_tensor("y_scr", [16, 128, TOK], BF, kind=dk).ap()

    R_win = [Res("win%d" % i) for i in range(10)]
    R_wout = [Res("wout%d" % i) for i in range(4)]
    R_wup = [Res("wup%d" % i) for i in range(16)]
    R_wdn = [Res("wdn%d" % i) for i in range(16)]
    R_q = [Res("qscr%d" % t) for t in range(8)]
    R_k = [Res("kscr%d" % t) for t in range(16)]
    R_v = [Res("vscr%d" % t) for t in range(16)]
    R_ya = [Res("ya%d" % t) for t in range(8)]
    R_yb = [[Res("yb%d_%d" % (h, t)) for t in range(8)] for h in range(8)]
    R_out = [[Res("out%d_%d" % (t, i)) for i in range(4)] for t in range(8)]

    psum_all = nc.alloc_psum_tensor("psum_all", [128, 8 * 512], F32)
    banks = [psum_all[:, i * 512:(i + 1) * 512] for i in range(8)]
    R_bank = [Res("bank%d" % i) for i in range(8)]

    def bank_bf(i):
        return banks[i].bitcast(BF)

    ident = nc.alloc_sbuf_tensor("ident", [128, 128], BF)
    ones_bf = nc.alloc_sbuf_tensor("ones_bf", [128, 128], BF)
    ones_f = nc.alloc_sbuf_tensor("ones_f", [128, 128], F32)
    tri = nc.alloc_sbuf_tensor("tri", [128, 128], BF)
    ctmp = nc.alloc_sbuf_tensor("ctmp", [128, 128], F32)
    gcol_in = nc.alloc_sbuf_tensor("gcol_in", [128, 16], F32)
    gcol_up = nc.alloc_sbuf_tensor("gcol_up", [128, 16], F32)
    gcol_sub = nc.alloc_sbuf_tensor("gcol_sub", [128, 1], F32)
    ones_col = nc.alloc_sbuf_tensor("ones_col", [128, 1], F32)
    eps_col = nc.alloc_sbuf_tensor("eps_col", [128, 1], F32)
    neg_lam = nc.alloc_sbuf_tensor("neg_lam", [128, 1], F32)
    lamt = nc.alloc_sbuf_tensor("lamt", [128, 4, 64], F32)
    lamp = nc.alloc_sbuf_tensor("lamp", [128, 2, 64], F32)
    lams = nc.alloc_sbuf_tensor("lams", [128, 2], F32)
    R_c = {n: Res(n) for n in "ident ones tri ctmp gcol_in gcol_up gcol_sub ones_col eps_col neg_lam lamt lamp lams".split()}

    b.memset("pool", ctmp[:], 1.0, [R_c["ctmp"]])
    S.op("pool", lambda e: e.affine_select(out=ctmp[:], in_=ctmp[:], pattern=[[1, 128]], compare_op=ALU.is_equal,
                                           fill=0.0, base=0, channel_multiplier=-1), [R_c["ctmp"]], [R_c["ctmp"]])
    b.cp("dve", ident[:], ctmp[:], [R_c["ctmp"]], [R_c["ident"]])
    b.memset("pool", ctmp[:], 1.0, [R_c["ctmp"]])
    S.op("pool", lambda e: e.affine_select(out=ctmp[:], in_=ctmp[:], pattern=[[1, 128]], compare_op=ALU.is_ge,
                                           fill=0.0, base=0, channel_multiplier=-1), [R_c["ctmp"]], [R_c["ctmp"]])
    b.cp("dve", tri[:], ctmp[:], [R_c["ctmp"]], [R_c["tri"]])
    b.memset("pool", ones_bf[:], 1.0, [R_c["ones"]])
    b.memset("pool", ones_f[:], 1.0, [R_c["ones"]])
    b.memset("pool", ones_col[:], 1.0, [R_c["ones_col"]])
    b.memset("pool", eps_col[:], EPS, [R_c["eps_col"]])
    b.dma("sp", gcol_in[:], g_pre_mix.rearrange("(c p) -> p c", p=128), [], [R_c["gcol_in"]], slow=True)
    b.dma("sp", gcol_up[:], g_pre_mlp.rearrange("(c p) -> p c", p=128), [], [R_c["gcol_up"]], slow=True)
    b.dma("sp", gcol_sub[:], subln.rearrange("(p o) -> p o", o=1), [], [R_c["gcol_sub"]], slow=True)
    b.tsc("dve", gcol_sub[:], gcol_sub[:], 1.0 - LAMBDA_INIT, None, ALU.mult, None, [R_c["gcol_sub"]], [R_c["gcol_sub"]])
    b.dma("sp", lamt[:], lam_in.partition_broadcast(128), [], [R_c["lamt"]])
    b.tt("dve", lamp[:, 0, :], lamt[:, 0, :], lamt[:, 1, :], ALU.mult, [R_c["lamt"]], [R_c["lamp"]])
    b.tt("dve", lamp[:, 1, :], lamt[:, 2, :], lamt[:, 3, :], ALU.mult, [R_c["lamt"]], [R_c["lamp"]])
    b.red("dve", lams[:], lamp[:], [R_c["lamp"]], [R_c["lams"]])
    b.act(lams[:], lams[:], AF.Exp, [R_c["lams"]], [R_c["lams"]])
    b.tsc("dve", neg_lam[:], lams[:, 1:2], -LAMBDA_INIT, None, ALU.add, None, [R_c["lams"]], [R_c["neg_lam"]])
    b.tt("dve", neg_lam[:], neg_lam[:], lams[:, 0:1], ALU.subtract, [R_c["lams"], R_c["neg_lam"]], [R_c["neg_lam"]])

    ar = Arena(nc, 200 * 1024)

    cast_units = []

    def add_units_rowmajor(src, dst, Rdst, ncols, colfn, rows_chunks, piece_of):
        for rc in range(rows_chunks):
            for u in range(ncols // 1024):
                s_ap = src[rc * 128:(rc + 1) * 128, u * 1024:(u + 1) * 1024]
                pieces = piece_of(rc, u)
                cast_units.append((s_ap, colfn(rc), [(dst[p, :, kc, :], Rdst[p]) for (p, kc) in pieces]))

    add_units_rowmajor(w_in, win_bf, R_win, INW, lambda rc: gcol_in[:, rc:rc + 1], 16,
                       lambda rc, u: [(2 * u, rc), (2 * u + 1, rc)])
    n_win_units = len(cast_units)
    add_units_rowmajor(w_out, wout_bf, R_wout, D, lambda rc: (None if rc < 8 else gcol_sub[:]), 16,
                       lambda rc, u: [(2 * u, rc), (2 * u + 1, rc)])
    add_units_rowmajor(w_up, wup_bf, R_wup, DFF, lambda rc: gcol_up[:, rc:rc + 1], 16,
                       lambda rc, u: [(2 * u, rc), (2 * u + 1, rc)])
    add_units_rowmajor(w_dn, wdn_bf, R_wdn, D, lambda rc: None, 64,
                       lambda rc, u: [((rc // 16) * 4 + 2 * u, rc % 16), ((rc // 16) * 4 + 2 * u + 1, rc % 16)])

    cast_state = {"i": 0}
    CS = 2
    cst_f = []
    cst_b = []

    def alloc_cast_staging():
        cst_f.clear()
        cst_b.clear()
        for i in range(CS):
            cst_f.append(ar.alloc("cst_f%d" % i, [1024], F32))
            cst_b.append(ar.alloc("cst_b%d" % i, [2, 512], BF))

    def do_casts(n, eng="dve"):
        for _ in range(n):
            i = cast_state["i"]
            if i >= len(cast_units):
                return
            cast_state["i"] = i + 1
            s_ap, col, dsts = cast_units[i]
            (f, rf) = cst_f[i % CS]
            (bt, rb) = cst_b[i % CS]
            b.dma("sp", f, s_ap, [], [rf])
            if col is None:
                b.cp("pool", bt.rearrange("p a b -> p (a b)"), f, [rf], [rb])
            elif eng == "act":
                b.act(bt.rearrange("p a b -> p (a b)"), f, AF.Copy, [rf, R_c["gcol_in"], R_c["gcol_up"], R_c["gcol_sub"], R_c["ones_col"]], [rb], scale=col)
            else:
                b.tsc(eng, bt.rearrange("p a b -> p (a b)"), f, col, None, ALU.mult, None,
                      [rf, R_c["gcol_in"], R_c["gcol_up"], R_c["gcol_sub"], R_c["ones_col"]], [rb])
            for k, (d_ap, d_res) in enumerate(dsts):
                b.dma("pool", d_ap, bt[:, k, :], [rb], [], pwrites=[d_res])

    ar.reset(0)
    alloc_cast_staging()
    cast_end = ar.off
    wmT = ar.alloc("wmT", [8, 128], BF)
    _off0 = ar.off
    wmf = ar.alloc("wmf", [8, 128], F32)
    wsb16 = ar.alloc("wsb16", [8, 128], BF)
    b.dma("sp", wmf[0], w_s.rearrange("g t s -> t g s"), [], [wmf[1]])
    b.cp("dve", wsb16[0], wmf[0], [wmf[1]], [wsb16[1]])
    for half in range(2):
        for g4 in range(4):
            g = half * 4 + g4
            b.tr(bank_bf(0)[:, g4 * 128:(g4 + 1) * 128], wsb16[0][:, g, :], ident[:], [wsb16[1], R_c["ident"]], [R_bank[0]])
        b.cp("dve", wmT[0][:, half * 4:half * 4 + 4, :], bank_bf(0)[:, 0:512].rearrange("p (g t) -> p g t", t=128),
             [R_bank[0]], [wmT[1]])
    for g in range(8):
        b.tt("pool", wmT[0][:, g, :], wmT[0][:, g, :], tri[:], ALU.mult, [wmT[1], R_c["tri"]], [wmT[1]])
    ar.reset(_off0)
    xin = [ar.alloc("xin%d" % i, [D], F32) for i in range(3)]
    junk = ar.alloc("junk", [D], BF)
    xs = [ar.alloc("xs%d" % i, [D], BF) for i in range(2)]
    xsT = []
    for s_ in range(2):
        v_, r_ = ar.alloc("xsT%d" % s_, [16, 512], BF)
        xsT.append((v_, ar.extra_res(r_, ["xsT%d_t%d" % (s_, t_) for t_ in range(4)])))
    wring = [ar.alloc("wring%d" % i, [16, 512], BF) for i in range(3)]
    uT = ar.alloc("uT", [8, 512], BF)
    vg = ar.alloc("vg", [4, 1024], F32)
    sqt = ar.alloc("sqt", [1024], F32)
    vln = [ar.alloc("vln%d" % i, [8, 128], BF) for i in range(4)]
    ctm = ar.alloc("ctm", [1024], F32)
    stg = [ar.alloc("stg%d" % i, [2048], BF) for i in range(4)]
    stg_ctr = {"i": 0}

    def next_stg():
        v_, r_ = stg[stg_ctr["i"] % 4]
        stg_ctr["i"] += 1
        return v_, r_
    bsb = ar.alloc("bsb", [1024], F32)
    lng = ar.alloc("lng", [1024], F32)
    lnb = ar.alloc("lnb", [1024], F32)
    ssA = [ar.alloc("ssA%d" % i, [4], F32) for i in range(2)]
    st1 = ar.alloc("st1", [32], F32)
    st2 = ar.alloc("st2", [32], F32)
    st3 = ar.alloc("st3", [32], F32)
    print("phase A arena bytes", ar.off)

    b.dma("sp", bsb[0], b_s.partition_broadcast(128), [], [bsb[1]])
    b.dma("sp", lng[0], ln_g.partition_broadcast(128), [], [lng[1]])
    b.dma("sp", lnb[0], ln_b.partition_broadcast(128), [], [lnb[1]])

    do_casts(n_win_units, eng="dve")
    CASTS_A = 6
    rest_per_iter = CASTS_A

    seglist = []
    for t in range(NSEG):
        seglist.append(("own", t))
        seglist.append(("oth", t))

    def seg_x(kind, t, tile):
        src = x_own if kind == "own" else x_oth
        r0 = t * SEG + tile * 128
        return src[r0:r0 + 128, :]

    xin_ctr = {"i": 0}

    xslots = {}

    def xL(U):
        i_, u_ = U // 8, U % 8
        if i_ >= N_SEG_A or U in xslots:
            return
        kind_, t_ = seglist[i_]
        xi, rxi = xin[xin_ctr["i"] % 3]
        xin_ctr["i"] += 1
        b.dma("act", xi, seg_x(kind_, t_, u_ % 4), [], [rxi])
        xslots[U] = (xi, rxi)

    def norm_items(i):
        kind, t = seglist[i]
        sl = i % 2
        ss, rss = ssA[sl]
        slots = xslots

        def useA(u_loc):
            u = 8 * i + u_loc
            xL(u)
            xL(u + 1)
            xL(u + 2)
            xi, rxi = slots[u]
            tile = u_loc % 4
            if u_loc < 4:
                b.act(junk[0], xi, AF.Square, [rxi], [junk[1], rss], accum_out=ss[:, tile:tile + 1])
                if u_loc == 3:
                    b.tsc("dve", ss, ss, 1.0 / D, EPS, ALU.mult, ALU.add, [rss], [rss])
                    b.act(ss, ss, AF.Sqrt, [rss], [rss])
                    b.recip(ss, ss, [rss], [rss])
                return
            xsb, rxs = xs[tile % 2]
            b.act(xsb, xi, AF.Copy, [rxi, rss], [rxs], scale=ss[:, tile:tile + 1])

        def useB(u_loc):
            tile = u_loc % 4
            xsb, rxs = xs[tile % 2]
            xtv, xtr = xsT[sl]
            for half in range(2):
                for c8 in range(8):
                    kc = half * 8 + c8
                    b.tr(bank_bf(half)[:, c8 * 128:(c8 + 1) * 128], xsb[:, kc * 128:(kc + 1) * 128], ident[:],
                         [rxs, R_c["ident"]], [R_bank[half]])
                dst = xtv[:, half * 8:half * 8 + 8, tile * 128:(tile + 1) * 128]
                src = bank_bf(half).rearrange("p (c t) -> p c t", t=128)
                b.act(dst, src, AF.Copy, [R_bank[half]], [xtr[tile]])

        items = []
        for u in range(4):
            items.append(lambda u=u: useA(u))
        items.append(lambda: useA(4))
        for u in range(4, 7):
            items.append(lambda u=u: (useB(u), useA(u + 1)))
        items.append(lambda: useB(7))
        return items

    pumpq = []

    def pump(n=1):
        for _ in range(n):
            if pumpq:
                pumpq.pop(0)()

    pieces_A = []
    for i_ in range(N_SEG_A):
        kind_, _t = seglist[i_]
        for pc_ in (range(10) if kind_ == "own" else range(6, 10)):
            pieces_A.append((win_bf[pc_], R_win[pc_]))
    stream = {"s": PieceStream(b, wring, pieces_A)}

    def load_piece_p(npump):
        pump(npump)
        return stream["s"].get()

    fb_ctr = {"i": 0}

    def next_fbank():
        i = 2 + (fb_ctr["i"] % 2)
        fb_ctr["i"] += 1
        return i

    tb_ctr = {"i": 0}

    def next_tbank():
        i = 4 + (tb_ctr["i"] % 4)
        tb_ctr["i"] += 1
        return i

    def fm_block(sl, wv, rw, cb, evac):
        bi = next_fbank()
        xtv, xtr = xsT[sl]
        for kc in range(16):
            b.mm(banks[bi][:, :], wv[:, kc, cb * 128:(cb + 1) * 128], xtv[:, kc, :],
                 kc == 0, kc == 15, [rw] + xtr, [R_bank[bi]])
        evac(bi)

    def tm_block(sl, wv, rw, tile, evac):
        bi = next_tbank()
        xtv, xtr = xsT[sl]
        for kc in range(16):
            b.mm(banks[bi][:, :], xtv[:, kc, tile * 128:(tile + 1) * 128], wv[:, kc, :], kc == 0, kc == 15,
                 [rw, xtr[tile]], [R_bank[bi]])
        evac(bi)

    def seg_compute(i):
        kind, t = seglist[i]
        sl = i % 2
        slot = t if kind == "own" else 8 + t
        NP = 2 if kind == "own" else 4
        if kind == "own":
            for pc in range(2):
                wv, rw = load_piece_p(NP)
                for cb in range(4):
                    g = pc * 4 + cb
                    fm_block(sl, wv, rw, cb, lambda bi, g=g: b.act(uT[0][:, g, :], banks[bi][:, :], AF.Gelu_apprx_tanh,
                                                                  [R_bank[bi]], [uT[1]]))
            for pc in range(2):
                wv, rw = load_piece_p(NP)
                for tile in range(4):
                    tm_block(sl, wv, rw, tile, lambda bi, tile=tile, pc=pc: b.act(
                        vg[0][:, tile, pc * 512:(pc + 1) * 512], banks[bi][:, :], AF.Gelu_apprx_tanh, [R_bank[bi]], [vg[1]]))
            vg3 = vg[0].rearrange("p a (g d) -> p (a g) d", d=128)
            b.red("dve", st1[0], vg3, [vg[1]], [st1[1]])
            for tile in range(4):
                b.tt("dve", sqt[0], vg[0][:, tile, :], vg[0][:, tile, :], ALU.mult, [vg[1]], [sqt[1]])
                b.red("dve", st2[0][:, tile * 8:(tile + 1) * 8], sqt[0].rearrange("p (g d) -> p g d", d=128), [sqt[1]], [st2[1]])
            b.tsc("dve", st1[0], st1[0], 1.0 / 128, None, ALU.mult, None, [st1[1]], [st1[1]])
            b.tt("dve", st3[0], st1[0], st1[0], ALU.mult, [st1[1]], [st3[1]])
            b.stt("dve", st2[0], st2[0], 1.0 / 128, st3[0], ALU.mult, ALU.subtract, [st2[1], st3[1]], [st2[1]])
            b.tsc("dve", st2[0], st2[0], EPS, None, ALU.add, None, [st2[1]], [st2[1]])
            b.act(st2[0], st2[0], AF.Sqrt, [st2[1]], [st2[1]])
            b.recip(st2[0], st2[0], [st2[1]], [st2[1]])
            for tile in range(4):
                vt = vg[0][:, tile, :].rearrange("p (g d) -> p g d", d=128)
                mb = st1[0][:, tile * 8:(tile + 1) * 8].unsqueeze(2).to_broadcast([128, 8, 128])
                rb_ = st2[0][:, tile * 8:(tile + 1) * 8].unsqueeze(2).to_broadcast([128, 8, 128])
                b.tt("dve", vt, vt, mb, ALU.subtract, [vg[1], st1[1]], [vg[1]])
                b.tt("dve", vt, vt, rb_, ALU.mult, [vg[1], st2[1]], [vg[1]])
                b.tt("dve", vg[0][:, tile, :], vg[0][:, tile, :], lng[0], ALU.mult, [vg[1], lng[1]], [vg[1]])
                vl, rvl = vln[tile]
                b.tt("dve", vl.rearrange("p g d -> p (g d)"), vg[0][:, tile, :], lnb[0], ALU.add, [vg[1], lnb[1]], [rvl])
            for pc in range(2):
                wv, rw = load_piece_p(NP)
                sv, sr = next_stg()
                sv3 = sv.rearrange("p (a b) -> p a b", b=512)
                for cb in range(4):
                    fm_block(sl, wv, rw, cb, lambda bi, cb=cb, sv3=sv3, sr=sr: b.act(sv3[:, cb, :], banks[bi][:, :], AF.Copy, [R_bank[bi]], [sr]))
                b.dma("pool", q_scr[pc * 4:(pc + 1) * 4, :, t * SEG:(t + 1) * SEG].rearrange("h p t -> p h t"), sv3, [sr], [],
                      pwrites=[R_q[t]])
        for pc in range(2):
            wv, rw = load_piece_p(NP)
            sv, sr = next_stg()
            sv3 = sv.rearrange("p (a b) -> p a b", b=512)
            for cb in range(4):
                fm_block(sl, wv, rw, cb, lambda bi, cb=cb, sv3=sv3, sr=sr: b.act(sv3[:, cb, :], banks[bi][:, :], AF.Copy, [R_bank[bi]], [sr]))
            b.dma("pool", k_scr[pc * 4:(pc + 1) * 4, :, slot * SEG:(slot + 1) * SEG].rearrange("h p t -> p h t"), sv3, [sr], [],
                  pwrites=[R_k[slot]])
        if kind == "own":
            for tile in range(4):
                vl, rvl = vln[tile]
                for half in range(2):
                    bi = next_fbank()
                    for g4 in range(4):
                        g = half * 4 + g4
                        b.mm(banks[bi][:, g4 * 128:(g4 + 1) * 128], vl[:, g, :], wmT[0][:, g, :], True, True,
                             [rvl, wmT[1]], [R_bank[bi]])
                    b.tt("dve", ctm[0][:, half * 512:(half + 1) * 512], banks[bi][:, :], bsb[0][:, half * 512:(half + 1) * 512],
                         ALU.add, [R_bank[bi], bsb[1]], [ctm[1]])
                if tile % 2 == 0:
                    ysv, ysr = next_stg()
                    ysv3 = ysv.rearrange("p (a b) -> p a b", b=256)
                b.tt("dve", ysv3[:, :, (tile % 2) * 128:(tile % 2 + 1) * 128], ctm[0].rearrange("p (g t) -> p g t", t=128),
                     uT[0][:, :, tile * 128:(tile + 1) * 128], ALU.mult, [ctm[1], uT[1]], [ysr])
                if tile % 2 == 1:
                    c_lo = t * SEG + (tile - 1) * 128
                    b.dma("pool", y_scr[0:8, :, c_lo:c_lo + 256].rearrange("c p t -> p c t"), ysv3, [ysr], [], pwrites=[R_ya[t]])
        for pc in range(2):
            wv, rw = load_piece_p(NP)
            sv, sr = next_stg()
            sv3 = sv.rearrange("p (a b) -> p a b", b=512)
            for tile in range(4):
                tm_block(sl, wv, rw, tile, lambda bi, tile=tile, sv3=sv3, sr=sr: b.act(
                    sv3[:, tile, :], banks[bi][:, :], AF.Copy, [R_bank[bi]], [sr]))
            for tile in range(4):
                b.dma("pool", v_scr[pc * 4:(pc + 1) * 4, :, slot * 4 + tile, :].rearrange("h p d -> p h d"),
                      sv3[:, tile, :].rearrange("p (h d) -> p h d", d=128), [sr], [], pwrites=[R_v[slot]])

    for it_ in norm_items(0):
        it_()
    for i in range(N_SEG_A):
        if i + 1 < N_SEG_A:
            pumpq.extend(norm_items(i + 1))
        ncast = rest_per_iter if PHASE_LIMIT != "A" else 0
        per = (ncast + 3) // 4
        for _ in range(4):
            pumpq.append(lambda per=per: do_casts(per, eng="dve"))
        seg_compute(i)
        pump(100)
    if PHASE_LIMIT == "A":
        S.emit({"pool": b.pool_dmas[-16:]})
        print("ops:", {e: len(S.ops[e]) for e in ENGS})
        return nc
    ar.reset(cast_end)
    kTb = [ar.alloc("kT%d" % i, [S_FULL], BF) for i in range(2)]
    Vb = [ar.alloc("V%d" % i, [64, 128], BF) for i in range(2)]
    qTb = [ar.alloc("qT%d" % i, [TOK], BF) for i in range(2)]
    Pb = [ar.alloc("P%d" % i, [2, 512], BF) for i in range(4)]
    e_r = [ar.alloc("e_r%d" % i, [512], F32) for i in range(2)]
    e_t = [ar.alloc("e_t%d" % i, [512], F32) for i in range(2)]
    e_o = ar.alloc("e_o", [512], F32)
    e_sq = ar.alloc("e_sq", [512], BF)
    e_rs = ar.alloc("e_rs", [512], F32)
    ybst = [ar.alloc("ybst%d" % i, [512], BF) for i in range(2)]
    Lacc = []
    for i_ in range(2):
        v_, r_ = ar.alloc("Lacc%d" % i_, [2, 512], F32)
        Lacc.append((v_, ar.extra_res(r_, ["Lacc%d_m%d" % (i_, m_) for m_ in range(2)])))
    e_ln = ar.alloc("e_ln", [512], F32)
    distt = ar.alloc("distt", [NDC], F32)
    biasT = ar.alloc("biasT", [8, 288], F32)
    bias0 = ar.alloc("bias0", [NDC - 288], F32)
    print("phase B arena bytes", ar.off)

    b.dma("sp", distt[0], dist_in, [], [distt[1]])
    for h in range(1, 8):
        b.tsc("dve", biasT[0][:, h, :], distt[0][:, 0:288], SLOPES[h], None, ALU.mult, None, [distt[1]], [biasT[1]])
    b.tsc("dve", bias0[0], distt[0][:, 288:NDC], SLOPES[0], None, ALU.mult, None, [distt[1]], [bias0[1]])

    SB = [(0, 1), (2, 3)]
    OBP = [(4, 5), (6, 7)]
    yb_ctr = {"i": 0}
    n_qt_total = sum(len(qtiles(h)) for h in range(8))
    casts_left = max(0, len(cast_units) - cast_state["i"])
    casts_per_qt = (casts_left + n_qt_total - 1) // n_qt_total

    def load_head(h):
        b.dma("sp", kTb[h % 2][0], k_scr[h], R_k, [kTb[h % 2][1]])
        b.dma("sp", Vb[h % 2][0], v_scr[h], R_v, [Vb[h % 2][1]])
        b.dma("sp", qTb[h % 2][0], q_scr[h], R_q, [qTb[h % 2][1]])

    tasks = []
    T_ = -1
    for h in range(8):
        for (t, c, w) in qtiles(h):
            T_ += 1
            blks = blocks_for(t, c, w)
            for i, blk in enumerate(blks):
                tasks.append((h, t, c, w, i, len(blks), blk, T_))

    def emit_qk(n):
        h, t, c, w, i, nb_, (kcol, vblk, c0, kind, tp, kb), T = tasks[n]
        kT, rkT = kTb[h % 2]
        qT, rqT = qTb[h % 2]
        sb = SB[n % 2]
        q0 = t * SEG + c
        nn = w - c0
        for m in range(2):
            b.mm(banks[sb[m]][:, 0:nn], kT[m * 64:(m + 1) * 64, kcol:kcol + 128],
                 qT[m * 64:(m + 1) * 64, q0 + c0:q0 + w], True, True, [rkT, rqT], [R_bank[sb[m]]])

    deferred = []

    OB = (4, 5)
    LB = (6, 7)
    e_l = [ar.alloc("e_l%d" % i, [512], F32) for i in range(2)]

    def epilogue0(n):
        h, t, c, w, i, nb_, blk, T = tasks[n]
        b.cp("dve", e_t[0][0][:, 0:w], banks[OB[0]][:, 0:w], [R_bank[OB[0]]], [e_t[0][1]])
        b.act(e_t[1][0][:, 0:w], banks[OB[1]][:, 0:w], AF.Copy, [R_bank[OB[1]]], [e_t[1][1]])
        b.cp("dve", e_l[0][0][:, 0:w], banks[LB[0]][:, 0:w], [R_bank[LB[0]]], [e_l[0][1]])
        b.act(e_l[1][0][:, 0:w], banks[LB[1]][:, 0:w], AF.Copy, [R_bank[LB[1]]], [e_l[1][1]])

    def epilogue1(n):
        h, t, c, w, i, nb_, blk, T = tasks[n]
        for m in (0, 1):
            b.recip(e_r[m][0][:, 0:w], e_l[m][0][:, 0:w], [e_l[m][1]], [e_r[m][1]])
        for m in (0, 1):
            b.tt("dve", e_t[m][0][:, 0:w], e_t[m][0][:, 0:w], e_r[m][0][:, 0:w], ALU.mult, [e_t[m][1], e_r[m][1]], [e_t[m][1]])
        b.stt("dve", e_o[0][:, 0:w], e_t[1][0][:, 0:w], neg_lam[:], e_t[0][0][:, 0:w], ALU.mult, ALU.add,
              [e_t[0][1], e_t[1][1], R_c["neg_lam"]], [e_o[1]])
        b.tt("dve", e_sq[0][:, 0:w], e_o[0][:, 0:w], e_o[0][:, 0:w], ALU.mult, [e_o[1]], [e_sq[1]])

    ssb = {"i": 0}

    def epilogue2(n):
        h, t, c, w, i, nb_, blk, T = tasks[n]
        q0 = t * SEG + c
        sbk = SB[(ssb["n"] + 1) % 2][0] if False else None
        bi = SB[ssb["cur"] % 2][0]
        b.mm(banks[bi][:, 0:w], ones_bf[:], e_sq[0][:, 0:w], True, True, [R_c["ones"], e_sq[1]], [R_bank[bi]])
        b.act(e_ln[0][:, 0:w], banks[bi][:, 0:w], AF.Ln, [R_bank[bi], R_c["eps_col"]], [e_ln[1]], bias=eps_col[:], scale=1.0 / 128)
        b.act(e_rs[0][:, 0:w], e_ln[0][:, 0:w], AF.Exp, [e_ln[1]], [e_rs[1]], scale=-0.5)
        yb, ryb = ybst[yb_ctr["i"] % 2]
        yb_ctr["i"] += 1
        b.tt("dve", yb[:, 0:w], e_o[0][:, 0:w], e_rs[0][:, 0:w], ALU.mult, [e_o[1], e_rs[1]], [ryb])
        b.dma("pool", y_scr[8 + h, :, q0:q0 + w], yb[:, 0:w], [ryb], [], pwrites=[R_yb[h][t]])
        do_casts(casts_per_qt, eng="dve")

    def emit_rest(n):
        h, t, c, w, i, nb_, (kcol, vblk, c0, kind, tp, kb), T = tasks[n]
        Vt, rV = Vb[h % 2]
        sb = SB[n % 2]
        Pt, rP = Pb[n % 4]
        nn = w - c0
        if h == 0:
            cc = cols[(256, t, c, i)] - 288
            bcol = bias0[0][:, cc:cc + 1]
            rbias = bias0[1]
        else:
            cc = cols[(512, t, c, i)]
            bcol = biasT[0][:, h, cc:cc + 1]
            rbias = biasT[1]
        for m in range(2):
            b.act(Pt[:, m, 0:nn], banks[sb[m]][:, 0:nn], AF.Exp, [R_bank[sb[m]], rbias], [rP], bias=bcol, scale=QK_SCALE)
        if kind == "diag":
            b.tt("dve", Pt[:, :, 0:128], Pt[:, :, 0:128], tri[:].unsqueeze(1).to_broadcast([128, 2, 128]), ALU.mult,
                 [rP, R_c["tri"]], [rP])
        for m in range(2):
            b.mm(banks[OB[m]][:, c0:w], Vt[:, vblk, :], Pt[:, m, 0:nn], i == 0, i == nb_ - 1, [rV, rP], [R_bank[OB[m]]])
            b.mm(banks[LB[m]][:, c0:w], ones_bf[:], Pt[:, m, 0:nn], i == 0, i == nb_ - 1, [R_c["ones"], rP], [R_bank[LB[m]]])
        if i == nb_ - 1:
            epilogue0(n)
            deferred.append((n + 3, lambda n=n: epilogue1(n)))
            deferred.append((n + 6, lambda n=n: epilogue2(n)))
        ssb["cur"] = n
        deferred.sort(key=lambda x: x[0])
        while deferred and deferred[0][0] <= n:
            deferred.pop(0)[1]()

    load_head(0)
    NT_ = len(tasks)
    emit_qk(0)
    for n in range(NT_):
        h = tasks[n][0]
        if tasks[n][4] == 0 and tasks[n][1] == 0 and tasks[n][2] == 0 and h + 1 < 8:
            load_head(h + 1)
        if n + 1 < NT_:
            emit_qk(n + 1)
        emit_rest(n)
    while deferred:
        deferred.pop(0)[1]()
    do_casts(10 ** 6, eng="dve")

    if PHASE_LIMIT == "AB":
        S.emit({"pool": b.pool_dmas[-16:]})
        print("ops:", {e: len(S.ops[e]) for e in ENGS})
        return nc
    ar.reset(0)
    wring = [ar.alloc("wringC%d" % i, [16, 512], BF) for i in range(3)]
    pieces_C = []
    for t_ in range(NSEG):
        for nb_ in range(4):
            pieces_C.append((wout_bf[nb_], R_wout[nb_]))
        for part_ in range(4):
            for pc_ in range(4):
                pieces_C.append((wup_bf[part_ * 4 + pc_], R_wup[part_ * 4 + pc_]))
            for nb_ in range(4):
                pieces_C.append((wdn_bf[part_ * 4 + nb_], R_wdn[part_ * 4 + nb_]))
    stream["s"] = PieceStream(b, wring, pieces_C)
    yT = ar.alloc("yT", [16, 512], BF)
    _yv, _yr = ar.alloc("yacc", [4, D], F32)
    yacc = (_yv, None)
    yaccR = ar.extra_res(_yr, ["yacc_t%d" % i for i in range(4)])
    xin = [ar.alloc("xinC%d" % i, [D], F32) for i in range(2)]
    junk = ar.alloc("junkC", [D], BF)
    hn = [ar.alloc("hn%d" % i, [D], BF) for i in range(2)]
    hnT = ar.alloc("hnT", [16, 512], BF)
    f1T = [ar.alloc("f1T%d" % i, [16, 512], BF) for i in range(2)]
    rtmp = [ar.alloc("rtmp%d" % i, [512], F32) for i in range(2)]
    gpost = ar.alloc("gpost", [D], F32)
    gmlp = ar.alloc("gmlp", [D], F32)
    ssq = ar.alloc("ssq", [16], F32)
    rs1 = ar.alloc("rs1", [4], F32)
    rs2 = [ar.alloc("rs2_%d" % i, [1], F32) for i in range(4)]
    rs3 = ar.alloc("rs3", [4], F32)
    print("phase C arena bytes", ar.off)
    b.dma("sp", gpost[0], g_post_mix.partition_broadcast(128), [], [gpost[1]])
    b.dma("sp", gmlp[0], g_post_mlp.partition_broadcast(128), [], [gmlp[1]])

    ub_ctr = {"i": 0}
    ob_ctr = {"i": 0}
    final_dmas = []

    def next_obank():
        i = 4 + ob_ctr["i"] % 3
        ob_ctr["i"] += 1
        return i

    def rstd_from(ssv, rss_):
        b.tsc("dve", ssv, ssv, 1.0 / D, EPS, ALU.mult, ALU.add, [rss_], [rss_])
        b.act(ssv, ssv, AF.Sqrt, [rss_], [rss_])
        b.recip(ssv, ssv, [rss_], [rss_])

    def load_yT(t):
        yres = [R_ya[t]] + [R_yb[h][t] for h in range(8)]
        b.dma("sp", yT[0], y_scr[:, :, t * SEG:(t + 1) * SEG].rearrange("c p t -> p c t"), yres, [yT[1]])

    load_yT(0)
    for t in range(NSEG):
        for nb in range(4):
            wv, rw = stream["s"].get()
            for tile in range(4):
                bi = next_obank()
                for kc in range(16):
                    b.mm(banks[bi][:, :], yT[0][:, kc, tile * 128:(tile + 1) * 128], wv[:, kc, :], kc == 0, kc == 15,
                         [yT[1], rw], [R_bank[bi]])
                b.cp("dve", yacc[0][:, tile, nb * 512:(nb + 1) * 512], banks[bi][:, :], [R_bank[bi]], [yaccR[tile]])
                b.act(junk[0][:, 0:512], yacc[0][:, tile, nb * 512:(nb + 1) * 512], AF.Square, [yaccR[tile]], [ssq[1]],
                      accum_out=ssq[0][:, tile * 4 + nb:tile * 4 + nb + 1])
        if t + 1 < NSEG:
            load_yT(t + 1)
        b.red("dve", rs1[0], ssq[0].rearrange("p (a b) -> p a b", b=4), [ssq[1]], [rs1[1]])
        rstd_from(rs1[0], rs1[1])
        xsl = {}

        def xl(tile, t=t):
            xi, rxi = xin[tile % 2]
            r0 = t * SEG + tile * 128
            b.dma("sp", xi, x_own[r0:r0 + 128, :], [], [rxi])
            xsl[tile] = (xi, rxi)

        xl(0)
        for tile in range(4):
            if tile + 1 < 4:
                xl(tile + 1)
            xi, rxi = xsl[tile]
            r0 = t * SEG + tile * 128
            b.stt("dve", yacc[0][:, tile, :], yacc[0][:, tile, :], rs1[0][:, tile:tile + 1], gpost[0], ALU.mult, ALU.mult,
                  [yaccR[tile], rs1[1], gpost[1]], [yaccR[tile]])
            b.tt("pool", yacc[0][:, tile, :], yacc[0][:, tile, :], xi, ALU.add, [yaccR[tile], rxi], [yaccR[tile]])
            b.dma("pool", out[r0:r0 + 128, :], yacc[0][:, tile, :], [yaccR[tile]], [R_out[t][tile]])
        for tile in range(4):
            r2c, rr2 = rs2[tile]
            b.act(junk[0], yacc[0][:, tile, :], AF.Square, [yaccR[tile]], [rr2], accum_out=r2c)
            b.tsc("dve", r2c, r2c, 1.0 / D, EPS, ALU.mult, ALU.add, [rr2], [rr2])
            b.act(r2c, r2c, AF.Sqrt, [rr2], [rr2])
            b.recip(r2c, r2c, [rr2], [rr2])
            hb, rhb = hn[tile % 2]
            b.act(hb, yacc[0][:, tile, :], AF.Copy, [yaccR[tile], rr2], [rhb], scale=r2c)
            for half in range(2):
                for c8 in range(8):
                    kc = half * 8 + c8
                    b.tr(bank_bf(half)[:, c8 * 128:(c8 + 1) * 128], hb[:, kc * 128:(kc + 1) * 128], ident[:],
                         [rhb, R_c["ident"]], [R_bank[half]])
                b.cp("dve", hnT[0][:, half * 8:half * 8 + 8, tile * 128:(tile + 1) * 128],
                     bank_bf(half).rearrange("p (c t) -> p c t", t=128), [R_bank[half]], [hnT[1]])
        for part in range(4):
            ft, rft = f1T[part % 2]
            for pc in range(4):
                wv, rw = stream["s"].get()
                for cb in range(4):
                    bi = 2 + ub_ctr["i"] % 2
                    ub_ctr["i"] += 1
                    for kc in range(16):
                        b.mm(banks[bi][:, :], wv[:, kc, cb * 128:(cb + 1) * 128], hnT[0][:, kc, :], kc == 0, kc == 15,
                             [rw, hnT[1]], [R_bank[bi]])
                    rt, rrt = rtmp[ub_ctr["i"] % 2]
                    b.act(rt, banks[bi][:, :], AF.Relu, [R_bank[bi]], [rrt])
                    b.tt("pool", ft[:, pc * 4 + cb, :], rt, rt, ALU.mult, [rrt], [rft])
            for nb in range(4):
                wv, rw = stream["s"].get()
                for tile in range(4):
                    bi = next_obank()
                    for fc in range(16):
                        b.mm(banks[bi][:, :], ft[:, fc, tile * 128:(tile + 1) * 128], wv[:, fc, :], fc == 0, fc == 15,
                             [rft, rw], [R_bank[bi]])
                    dst = yacc[0][:, tile, nb * 512:(nb + 1) * 512]
                    if part == 0:
                        b.cp("dve", dst, banks[bi][:, :], [R_bank[bi]], [yaccR[tile]])
                    else:
                        b.tt("dve", dst, dst, banks[bi][:, :], ALU.add, [R_bank[bi], yaccR[tile]], [yaccR[tile]])
        for tile in range(4):
            b.act(junk[0], yacc[0][:, tile, :], AF.Square, [yaccR[tile]], [rs3[1]], accum_out=rs3[0][:, tile:tile + 1])
        rstd_from(rs3[0], rs3[1])
        for tile in range(4):
            xi, rxi = xin[tile % 2]
            r0 = t * SEG + tile * 128
            b.dma("sp", xi, out[r0:r0 + 128, :], [R_out[t][tile]], [rxi])
            b.stt("dve", yacc[0][:, tile, :], yacc[0][:, tile, :], rs3[0][:, tile:tile + 1], gmlp[0], ALU.mult, ALU.mult,
                  [yaccR[tile], rs3[1], gmlp[1]], [yaccR[tile]])
            b.tt("dve", xi, xi, yacc[0][:, tile, :], ALU.add, [yaccR[tile], rxi], [rxi])
            final_dmas.append(b.dma("pool", out[r0:r0 + 128, :], xi, [rxi], [R_out[t][tile]]))

    S.emit({"pool": final_dmas})
    print("ops:", {e: len(S.ops[e]) for e in ENGS})
    return nc


_CACHE = {}


def kernel(**inputs):
    x = np.asarray(inputs["x"], dtype=np.float32)
    L0 = lambda k: np.ascontiguousarray(np.asarray(inputs[k], dtype=np.float32)[0])
    w_in = L0("w_in")
    idx_u = np.concatenate([np.arange(g * 256, g * 256 + 128) for g in range(8)])
    idx_v = idx_u + 128
    perm = np.concatenate([idx_u, idx_v, np.arange(2048, 5120)])
    w_in_p = np.ascontiguousarray(w_in[:, perm])
    lam_in = np.stack([L0("lambda_q1"), L0("lambda_k1"), L0("lambda_q2"), L0("lambda_k2")]).astype(np.float32)
    common = {
        "w_in": w_in_p, "w_out": L0("w_out"), "w_up": L0("w_up"), "w_dn": L0("w_down"),
        "g_pre_mix": L0("pre_mix_g"), "g_post_mix": L0("post_mix_g"), "g_pre_mlp": L0("pre_mlp_g"),
        "g_post_mlp": L0("post_mlp_g"),
        "ln_g": L0("gmlp_ln_g").reshape(1024), "ln_b": L0("gmlp_ln_b").reshape(1024),
        "w_s": L0("gmlp_w_s"), "b_s": L0("gmlp_b_s").reshape(1024),
        "lam_in": lam_in, "subln": L0("diff_subln_g"),
    }
    in_maps = []
    for c in range(8):
        bb, j = c // 2, c % 2
        xb = x[bb].reshape(16, SEG, D)
        m = dict(common)
        m["x_own"] = np.ascontiguousarray(xb[OWN[j]].reshape(TOK, D))
        m["x_oth"] = np.ascontiguousarray(xb[OWN[1 - j]].reshape(TOK, D))
        m["dist"] = dist_table(j)
        in_maps.append(m)
    if "nc" not in _CACHE:
        _CACHE["nc"] = build_program()
    res = run_bass_kernel_spmd(_CACHE["nc"], in_maps, core_ids=list(range(8)))
    outp = np.empty((4, S_FULL, D), np.float32)
    for c in range(8):
        bb, j = c // 2, c % 2
        o = np.asarray(res.results[c]["out"], dtype=np.float32).reshape(8, SEG, D)
        ov = outp[bb].reshape(16, SEG, D)
        for t, s in enumerate(OWN[j]):
            ov[s] = o[t]
    _CACHE["last"] = res
    return outp
```
